# Optimizing a Trainium2 kernel written in Bass

```python
import jax, jax.numpy as jnp
from jax import lax
import numpy as np

D_MODEL = 1024
BATCH = 8
SEQ = 2048
DEPTH = 1
DEC_BATCH = 16
DEC_SEQ = 4096
PAST_LEN = 128

MIX_WIDTH = D_MODEL
ATTN_WIDTH = MIX_WIDTH // 2
HEAD_DIM = 64
N_HEADS = ATTN_WIDTH // HEAD_DIM
N_KV_HEADS = 2
KV_GROUP = N_HEADS // N_KV_HEADS
KV_WIDTH = N_KV_HEADS * HEAD_DIM
WINDOW = 128
ATTN_BLOCK = 128
HG_WIDTH = MIX_WIDTH - ATTN_WIDTH
HG_EXPAND = 128
HG_HEADS = HG_WIDTH // HG_EXPAND
HG_DK = HG_EXPAND
HG_DV = HG_WIDTH // HG_HEADS
HG_KDIM = HG_HEADS * HG_DK
HG_CHUNK = 64
D_FF = 2816
ALPHA = (2.0 * DEPTH) ** 0.25
BETA_INIT = (8.0 * DEPTH) ** -0.25
EPS = 1e-5
IN_SIZES = (ATTN_WIDTH, KV_WIDTH, KV_WIDTH, HG_KDIM, HG_KDIM, HG_KDIM, HG_WIDTH, HG_WIDTH)
IN_COLS = ATTN_WIDTH + 2 * KV_WIDTH + 3 * HG_KDIM + 2 * HG_WIDTH

kernel_name = "hymba_swa_hgrn2_macaron_deepnorm_encoder"


def layer_norm(x, g, b):
    xf = x.astype(jnp.float32)
    mu = jnp.mean(xf, axis=-1, keepdims=True)
    var = jnp.mean(jnp.square(xf - mu), axis=-1, keepdims=True)
    y = (xf - mu) * lax.rsqrt(var + EPS) * g.astype(jnp.float32) + b.astype(jnp.float32)
    return y.astype(x.dtype)


def group_rms_norm(x, g, n_groups):
    shp = x.shape
    xf = x.astype(jnp.float32).reshape(shp[:-1] + (n_groups, shp[-1] // n_groups))
    xf = xf * lax.rsqrt(jnp.mean(jnp.square(xf), axis=-1, keepdims=True) + EPS)
    return (xf.reshape(shp) * g.astype(jnp.float32)).astype(x.dtype)


def swiglu_ffn(x, w13, w2):
    gate, up = jnp.split(x @ w13, 2, axis=-1)
    return (jax.nn.silu(gate) * up) @ w2


def split_in_proj(proj):
    outs, off = [], 0
    for s in IN_SIZES:
        outs.append(proj[..., off:off + s])
        off += s
    return outs


def windowed_gqa_attention(q, k, v, sink):
    B, T = q.shape[0], q.shape[1]
    nb = T // ATTN_BLOCK
    span = ATTN_BLOCK + 2 * WINDOW
    qb = jnp.moveaxis(q.reshape(B, nb, ATTN_BLOCK, N_KV_HEADS, KV_GROUP, HEAD_DIM), 1, 0)
    pad = ((0, 0), (WINDOW, WINDOW), (0, 0), (0, 0))
    kp = jnp.pad(k, pad)
    vp = jnp.pad(v, pad)
    slopes = 2.0 ** (-8.0 * jnp.arange(1, N_HEADS + 1, dtype=jnp.float32) / N_HEADS)
    slopes = slopes.reshape(1, N_KV_HEADS, KV_GROUP, 1, 1)
    sink5 = sink.astype(jnp.float32).reshape(1, N_KV_HEADS, KV_GROUP, 1, 1)
    scale = HEAD_DIM ** -0.5
    offs_q = jnp.arange(ATTN_BLOCK)
    offs_k = jnp.arange(span) - WINDOW
    rel = offs_k[None, :] - offs_q[:, None]
    alibi = slopes * jnp.abs(rel).astype(jnp.float32)

    def block(args):
        qi, i = args
        start = i * ATTN_BLOCK
        ki = lax.dynamic_slice_in_dim(kp, start, span, axis=1)
        vi = lax.dynamic_slice_in_dim(vp, start, span, axis=1)
        s_pos = start + offs_k
        valid = (jnp.abs(rel) <= WINDOW) & ((s_pos >= 0) & (s_pos < T))[None, :]
        logits = jnp.einsum('bqkgd,bskd->bkgqs', qi, ki,
                            preferred_element_type=jnp.float32) * scale - alibi
        logits = jnp.where(valid, logits, -jnp.inf)
        sink_col = jnp.broadcast_to(sink5, logits.shape[:-1] + (1,))
        p = jax.nn.softmax(jnp.concatenate([logits, sink_col], axis=-1), axis=-1)[..., :-1]
        return jnp.einsum('bkgqs,bskd->bqkgd', p.astype(vi.dtype), vi)

    out = lax.map(block, (qb, jnp.arange(nb)))
    return jnp.moveaxis(out, 0, 1).reshape(B, T, N_HEADS * HEAD_DIM)


def gla_chunk_scan(q, k, v, g):
    B, T, H, dk = q.shape
    dv = v.shape[-1]
    n = T // HG_CHUNK

    def to_chunks(a):
        return jnp.moveaxis(a.reshape(B, n, HG_CHUNK, H, a.shape[-1]), 1, 0)

    causal = jnp.tril(jnp.ones((HG_CHUNK, HG_CHUNK), dtype=bool))[None, :, :, None, None]

    def step(S, xs):
        qc, kc, vc, gc = xs
        b = jnp.cumsum(gc, axis=1)
        diff = b[:, :, None] - b[:, None, :]
        decay = jnp.exp(jnp.where(causal, diff, -jnp.inf))
        A = jnp.einsum('bthd,bshd,btshd->bhts', qc, kc, decay)
        o_intra = jnp.einsum('bhts,bshv->bthv', A, vc)
        o_inter = jnp.einsum('bthd,bhdv->bthv', qc * jnp.exp(b), S)
        b_last = b[:, -1]
        k_dec = kc * jnp.exp(b_last[:, None] - b)
        S_new = jnp.exp(b_last)[..., None] * S + jnp.einsum('bshd,bshv->bhdv', k_dec, vc)
        return S_new, o_intra + o_inter

    S0 = jnp.zeros((B, H, dk, dv), jnp.float32)
    _, o = lax.scan(step, S0, (to_chunks(q), to_chunks(k), to_chunks(v), to_chunks(g)))
    return jnp.moveaxis(o, 0, 1).reshape(B, T, H, dv)


def hgrn2_gate(f_logit, lb_param, layer):
    lb = jnp.cumsum(jax.nn.softmax(lb_param.astype(jnp.float32), axis=0), axis=0)[layer]
    s = jax.nn.sigmoid(f_logit.astype(jnp.float32))
    log_f = jnp.log(lb + (1.0 - lb) * s)
    k = (1.0 - lb) * (1.0 - s)
    return log_f, k


def hgrn2_bidirectional(q, f_fwd, f_bwd, i_in, lb_fwd, lb_bwd, layer):
    B, T = q.shape[0], q.shape[1]
    heads = lambda a, d: a.astype(jnp.float32).reshape(B, T, HG_HEADS, d)
    qh = heads(q, HG_DK)
    vh = heads(i_in, HG_DV)
    gf, kf = hgrn2_gate(f_fwd, lb_fwd, layer)
    gb, kb = hgrn2_gate(f_bwd, lb_bwd, layer)
    gf, kf, gb, kb = heads(gf, HG_DK), heads(kf, HG_DK), heads(gb, HG_DK), heads(kb, HG_DK)
    o_f = gla_chunk_scan(qh, kf, vh, gf)
    flip = lambda a: jnp.flip(a, axis=1)
    o_b = flip(gla_chunk_scan(flip(qh), flip(kb), flip(vh), flip(gb)))
    return (o_f + o_b).reshape(B, T, HG_WIDTH)


def token_mixer(h, w_in, w_out, sink, attn_norm_g, lb_fwd, lb_bwd, hg_norm_g, layer):
    B, T, _ = h.shape
    q_a, k_a, v_a, q_h, f_f, f_b, i_h, g_h = split_in_proj(h @ w_in)
    o_attn = windowed_gqa_attention(q_a.reshape(B, T, N_HEADS, HEAD_DIM),
                                    k_a.reshape(B, T, N_KV_HEADS, HEAD_DIM),
                                    v_a.reshape(B, T, N_KV_HEADS, HEAD_DIM), sink)
    o_attn = group_rms_norm(o_attn, attn_norm_g, 1)
    o_hg = hgrn2_bidirectional(q_h, f_f, f_b, i_h, lb_fwd, lb_bwd, layer).astype(h.dtype)
    o_hg = group_rms_norm(o_hg, hg_norm_g, HG_HEADS) * jax.nn.silu(g_h)
    return jnp.concatenate([o_attn, o_hg], axis=-1) @ w_out


def trunk(x, ln_g, ln_b, ffn_w13, ffn_w2, w_in, attn_sink, attn_norm_g,
          hg_lb_fwd, hg_lb_bwd, hg_norm_g, w_out):
    for l in range(DEPTH):
        x = layer_norm(ALPHA * x + 0.5 * swiglu_ffn(x, ffn_w13[l, 0], ffn_w2[l, 0]),
                       ln_g[l, 0], ln_b[l, 0])
        mix = token_mixer(x, w_in[l], w_out[l], attn_sink[l], attn_norm_g[l],
                          hg_lb_fwd, hg_lb_bwd, hg_norm_g[l], l)
        x = layer_norm(ALPHA * x + mix, ln_g[l, 1], ln_b[l, 1])
        x = layer_norm(ALPHA * x + 0.5 * swiglu_ffn(x, ffn_w13[l, 1], ffn_w2[l, 1]),
                       ln_g[l, 2], ln_b[l, 2])
    return x


def setup_inputs(seed: int = 0) -> dict:
    key = jax.random.key(seed)
    ks = jax.random.split(key, 16)
    f32 = jnp.float32
    nrm = lambda k, shp, s: jax.random.normal(k, shp, f32) * s
    x_prompt = jax.random.normal(ks[0], (BATCH, SEQ, D_MODEL), f32)
    x_sample = jax.random.normal(ks[1], (DEC_BATCH, DEC_SEQ, D_MODEL), f32)
    ln_g = 1.0 + nrm(ks[2], (DEPTH, 3, D_MODEL), 0.02)
    ln_b = nrm(ks[3], (DEPTH, 3, D_MODEL), 0.02)
    ffn_w13 = nrm(ks[4], (DEPTH, 2, D_MODEL, 2 * D_FF), D_MODEL ** -0.5)
    ffn_w2 = nrm(ks[5], (DEPTH, 2, D_FF, D_MODEL), BETA_INIT * D_FF ** -0.5)
    col_scale = jnp.concatenate([
        jnp.ones((ATTN_WIDTH + KV_WIDTH,), f32),
        jnp.full((KV_WIDTH,), BETA_INIT, f32),
        jnp.ones((3 * HG_KDIM,), f32),
        jnp.full((HG_WIDTH,), BETA_INIT, f32),
        jnp.ones((HG_WIDTH,), f32)])
    w_in = nrm(ks[6], (DEPTH, D_MODEL, IN_COLS), D_MODEL ** -0.5) * col_scale
    attn_sink = nrm(ks[7], (DEPTH, N_HEADS), 0.5)
    attn_norm_g = 1.0 + nrm(ks[8], (DEPTH, ATTN_WIDTH), 0.02)
    hg_lb_fwd = nrm(ks[9], (DEPTH + 1, HG_KDIM), 0.5)
    hg_lb_bwd = nrm(ks[10], (DEPTH + 1, HG_KDIM), 0.5)
    hg_norm_g = 1.0 + nrm(ks[11], (DEPTH, HG_WIDTH), 0.02)
    w_out = nrm(ks[12], (DEPTH, MIX_WIDTH, D_MODEL), BETA_INIT * MIX_WIDTH ** -0.5)
    return {"x_prompt": x_prompt, "x_sample": x_sample, "ln_g": ln_g, "ln_b": ln_b,
            "ffn_w13": ffn_w13, "ffn_w2": ffn_w2, "w_in": w_in, "attn_sink": attn_sink,
            "attn_norm_g": attn_norm_g, "hg_lb_fwd": hg_lb_fwd, "hg_lb_bwd": hg_lb_bwd,
            "hg_norm_g": hg_norm_g, "w_out": w_out}


def reference(x_prompt, x_sample, ln_g, ln_b, ffn_w13, ffn_w2, w_in, attn_sink,
              attn_norm_g, hg_lb_fwd, hg_lb_bwd, hg_norm_g, w_out):
    y_prompt = trunk(x_prompt, ln_g, ln_b, ffn_w13, ffn_w2, w_in, attn_sink, attn_norm_g,
                     hg_lb_fwd, hg_lb_bwd, hg_norm_g, w_out)
    y_sample = trunk(x_sample, ln_g, ln_b, ffn_w13, ffn_w2, w_in, attn_sink, attn_norm_g,
                     hg_lb_fwd, hg_lb_bwd, hg_norm_g, w_out)
    return (y_prompt, y_sample)
```

```python
import numpy as np
from contextlib import ExitStack
import concourse.bass as bass
import concourse.mybir as mybir
from concourse.bass_utils import run_bass_kernel_spmd

F32 = mybir.dt.float32
BF16 = mybir.dt.bfloat16
AF = mybir.ActivationFunctionType
ALU = mybir.AluOpType

D = 1024
DFF = 2816
NF = DFF // 128
INC = 3328
ALPHA = 2.0 ** 0.25
EPS = 1e-5
ENGS = ("pe", "act", "dve", "pool", "sp")


import heapq


class _Op:
    __slots__ = ("fn", "eng", "waits", "ctr", "is_dma", "preds", "succs", "cost", "lat", "idx", "npred", "ready", "ev", "opts", "done", "bulk")


class _Rec:
    def __getattr__(self, name):
        def f(*a, **kw):
            object.__setattr__(self, "iname", name)
            object.__setattr__(self, "kw", kw)
            return self
        return f


_F32 = mybir.dt.float32
SYNC_LAT = 0.25
PE_COL = 1.0 / 2400.0


def _estimate(eng, fn, dma):
    r = _Rec()
    fn(r)
    kw = r.kw
    name = r.iname
    if dma:
        nb = 0
        for k in ("out", "in_"):
            try:
                nb = max(nb, int(kw[k].nbytes))
            except Exception:
                pass
        return (0.7 if eng == "pool" else 0.45), 2.5 + nb / 120e3
    n = 1
    for k in ("out", "in_", "in0", "in1", "data0", "data1", "rhs"):
        a = kw.get(k)
        if a is not None and hasattr(a, "shape"):
            m = 1
            for s in list(a.shape)[1:]:
                m *= int(s)
            if k == "rhs" or name not in ("matmul", "transpose"):
                n = max(n, m)
    if name == "matmul":
        passes = 4 if kw["rhs"].dtype == _F32 else 1
        return max(0.07, 0.03 + passes * n * PE_COL * _PE_SLOW[0]), 0.0
    if name == "transpose":
        passes = 4 if kw["in_"].dtype == _F32 else 1
        return max(0.07, 0.03 + passes * 128 * PE_COL * _PE_SLOW[0]), 0.0
    if eng == "act":
        return 0.12 + n * 0.001, 0.0
    if eng == "dve":
        if name == "reciprocal":
            return 0.1 + n * 0.0065, 0.0
        return 0.1 + n * 0.00115, 0.0
    return 0.2 + n * 0.002, 0.0


_PE_SLOW = [1.0]


class Sched:
    N_WSEMS = 10
    N_SSEMS = 8

    def __init__(self, n_dma_sems=32):
        self.wdma_rr = 0
        self.sdma_rr = 0
        self.bulk_hist = []
        self.ops = {e: [] for e in ENGS}
        self.clock = {e: {} for e in ENGS}
        self.count = {}
        self.last_w = {}
        self.readers = {}
        self.n_dma_sems = n_dma_sems
        self.dma_rr = 0
        self.dma_last = {}
        self.pending = []
        self.phase_times = []

    def _need(self, eng, deps):
        clk = self.clock[eng]
        best = {}
        for (c, v, s) in deps:
            if clk.get(c, 0) >= v:
                continue
            if c not in best or best[c][0] < v:
                best[c] = (v, s)
        items = list(best.items())
        waits = []
        for c, (v, s) in items:
            implied = False
            for c2, (v2, s2) in items:
                if c2 != c and s2.get(c, 0) >= v:
                    implied = True
                    break
            if not implied:
                waits.append((c, v))
        for c, (v, s) in items:
            for cc, vv in s.items():
                if clk.get(cc, 0) < vv:
                    clk[cc] = vv
            if clk.get(c, 0) < v:
                clk[c] = v
        return waits

    def add(self, eng, fn, reads=(), writes=(), dma=False, bulk=False):
        op = _Op()
        op.is_dma = dma
        op.bulk = bulk
        if isinstance(eng, tuple):
            op.opts = {e: (fn[e],) + _estimate(e, fn[e], dma) for e in eng}
            op.eng = None
            op.fn = None
        else:
            c_, l_ = _estimate(eng, fn, dma)
            if bulk:
                l_ = 2.5 + (l_ - 2.5) * 1.5
            op.opts = {eng: (fn, c_, l_)}
            op.eng = eng
            op.fn = fn
        preds = {}
        for k in reads:
            w = self.last_w.get(k)
            if w is not None:
                preds[id(w)] = w
        for k in writes:
            w = self.last_w.get(k)
            if w is not None:
                preds[id(w)] = w
            for r_ in self.readers.get(k, ()):
                preds[id(r_)] = r_
        if bulk:
            if len(self.bulk_hist) >= self.N_WSEMS:
                w = self.bulk_hist[-self.N_WSEMS]
                preds[id(w)] = w
            self.bulk_hist.append(op)
        preds.pop(id(op), None)
        op.preds = list(preds.values())
        op.succs = []
        for k in reads:
            self.readers.setdefault(k, []).append(op)
        for k in writes:
            self.last_w[k] = op
            self.readers[k] = []
        self.pending.append(op)
        return op

    def _list_schedule(self, ops):
        for i, op in enumerate(ops):
            op.idx = i
            op.npred = len(op.preds)
            op.ready = 0.0
            op.done = False
        for op in ops:
            for p in op.preds:
                p.succs.append(op)
        fut = {e: [] for e in ENGS}
        rdy = {e: [] for e in ENGS}
        free = {e: 0.0 for e in ENGS}
        for op in ops:
            if op.npred == 0:
                for e in op.opts:
                    heapq.heappush(fut[e], (0.0, op.idx))
        order = []
        nleft = len(ops)
        while nleft:
            bt, be = None, None
            for e in ENGS:
                while rdy[e] and ops[rdy[e][0]].done:
                    heapq.heappop(rdy[e])
                while fut[e] and ops[fut[e][0][1]].done:
                    heapq.heappop(fut[e])
                if rdy[e]:
                    t = free[e]
                elif fut[e]:
                    t = max(free[e], fut[e][0][0])
                else:
                    continue
                if bt is None or t < bt:
                    bt, be = t, e
            assert be is not None, "dependency cycle"
            e, t = be, bt
            while fut[e] and fut[e][0][0] <= t:
                _, i = heapq.heappop(fut[e])
                if not ops[i].done:
                    heapq.heappush(rdy[e], i)
            op = ops[heapq.heappop(rdy[e])]
            op.done = True
            op.eng = e
            op.fn, op.cost, op.lat = op.opts[e]
            if op.is_dma:
                free[e] = t + op.cost
                fin = free[e] + op.lat
            else:
                fin = t + op.cost
                free[e] = fin
            order.append(op)
            nleft -= 1
            for s in op.succs:
                r_ = fin + (SYNC_LAT if (s.eng != e or op.is_dma) else 0.0)
                if r_ > s.ready:
                    s.ready = r_
                s.npred -= 1
                if s.npred == 0:
                    for e2 in s.opts:
                        heapq.heappush(fut[e2], (s.ready, s.idx))
        self.sim_time = max(free.values()) if order else 0.0
        return order

    def _commit(self, op):
        eng = op.eng
        deps = [p.ev for p in op.preds]
        if eng == "pe":
            deps = [d for d in deps if d[0] != "pe"]
        if op.is_dma:
            if op.bulk:
                ctr = "wdma%d" % self.wdma_rr
                self.wdma_rr = (self.wdma_rr + 1) % self.N_WSEMS
            elif eng == "pool":
                ctr = "sdma%d" % self.sdma_rr
                self.sdma_rr = (self.sdma_rr + 1) % self.N_SSEMS
            else:
                ctr = "dma%d" % self.dma_rr
                self.dma_rr = (self.dma_rr + 1) % self.n_dma_sems
            prev = self.dma_last.get(ctr)
            if prev is not None:
                deps.append(prev)
            step = 16
        else:
            ctr = eng
            step = 1
        op.waits = self._need(eng, deps)
        val = self.count.get(ctr, 0) + step
        self.count[ctr] = val
        op.ctr = ctr
        snap = dict(self.clock[eng])
        snap[ctr] = val
        op.ev = (ctr, val, snap)
        if op.is_dma:
            self.dma_last[ctr] = op.ev
        self.ops[eng].append(op)

    def flush(self):
        ops, self.pending = self.pending, []
        if not ops:
            return
        for op in self._list_schedule(ops):
            self._commit(op)
            op.preds = None
            op.succs = None
        busy = {}
        for op in ops:
            busy[op.eng] = busy.get(op.eng, 0.0) + op.cost
        self.phase_times.append((len(ops), round(self.sim_time, 1), {k: round(v, 1) for k, v in busy.items()}))

    def barrier(self):
        self.flush()
        allc = [(c, v, {}) for c, v in self.count.items()]
        for eng in ENGS:
            op = _Op()
            op.fn = None
            op.is_dma = False
            op.ctr = None
            op.waits = self._need(eng, allc)
            self.ops[eng].append(op)
        self.last_w = {}
        self.readers = {}
        self.bulk_hist = []

    def emit(self, block, sems):
        engobj = {"pe": "tensor", "act": "scalar", "dve": "vector", "pool": "gpsimd", "sp": "sync"}

        def mk(engname):
            def body(e):
                for op in self.ops[engname]:
                    for (c, v) in op.waits:
                        e.wait_ge(sems[c], v)
                    if op.fn is None:
                        continue
                    ins = op.fn(e)
                    ins.then_inc(sems[op.ctr], 16 if op.is_dma else 1)
            return body

        for engname in ENGS:
            getattr(block, engobj[engname])(mk(engname))


def load_w_bf16(S, dst, src_rows, nk, key):
    for k in range(nk):
        S.add("pool", lambda e, k=k: e.dma_start(out=dst[:, k, :], in_=src_rows[k * 128:(k + 1) * 128, :],
                                                 max_dma_last_dim=4096),
              writes=[(key, k)], dma=True, bulk=True)


def layer_norm_tile(S, xr, xkey, py_halves, pykeys, res_scale, eps, gt, bt, stats, mv, rstd, nmr, skey, mhalf, affine=True):
    for h in range(2):
        sl = slice(h * 512, (h + 1) * 512)
        S.add("dve", lambda e, h=h, sl=sl: e.scalar_tensor_tensor(out=xr[:, sl], in0=xr[:, sl], scalar=res_scale,
                                                                 in1=py_halves[h], op0=ALU.mult, op1=ALU.add),
              reads=[pykeys[h]], writes=[xkey])
        S.add("dve", lambda e, h=h, sl=sl: e.bn_stats(out=stats[:, h, :], in_=xr[:, sl]),
              reads=[xkey], writes=[skey])
    S.add("dve", lambda e: e.bn_aggr(out=mv[:], in_=stats[:].rearrange("p a b -> p (a b)")), reads=[skey], writes=[skey])
    S.add("dve", lambda e: e.tensor_scalar(out=rstd[:], in0=mv[:, 1:2], scalar1=eps, scalar2=None,
                                           op0=ALU.add), reads=[skey], writes=[skey])
    S.add("pool", lambda e: e.tensor_tensor(out=rstd[:], in0=rstd[:], in1=mhalf[:, 0:1], op=ALU.pow),
          reads=[skey, "mhalf"], writes=[skey])
    S.add("dve", lambda e: e.scalar_tensor_tensor(out=nmr[:], in0=mv[:, 0:1], scalar=-1.0, in1=rstd[:],
                                                  op0=ALU.mult, op1=ALU.mult), reads=[skey], writes=[skey])
    S.add("act", lambda e: e.activation(out=xr[:], in_=xr[:], func=AF.Identity, bias=nmr[:], scale=rstd[:]),
          reads=[skey], writes=[xkey])
    if affine:
        S.add("dve", lambda e: e.tensor_tensor(out=xr[:], in0=xr[:], in1=gt[:], op=ALU.mult), reads=["gt"], writes=[xkey])
        S.add("pool", lambda e: e.tensor_tensor(out=xr[:], in0=xr[:], in1=bt[:], op=ALU.add), reads=["bt"], writes=[xkey])


W13_BLOCKS = (0, 5, 6, 1, 7, 2, 8, 3, 9, 4, 10)


def ffn_alloc_weights(nc, st, tag):
    w13s = st.enter_context(nc.sbuf_tensor(tag + "w13s", [128, 8, 2 * DFF], BF16))
    w2s = st.enter_context(nc.sbuf_tensor(tag + "w2s", [128, NF, D], BF16))
    return w13s, w2s


def ffn_load_weights(S, w13s, w2s, w13, w2, parts=(0, 1)):
    for i, blk in enumerate(W13_BLOCKS if 0 in parts else ()):
        c0 = blk * 512
        S.add("pool", lambda e, c0=c0: e.dma_start(out=w13s[:, :, c0:c0 + 512],
                                                  in_=w13[:, c0:c0 + 512].rearrange("(k p) c -> p k c", p=128)),
              writes=[("w13s", blk)], dma=True, bulk=True)
    for j in range(NF if 1 in parts else 0):
        S.add("pool", lambda e, j=j: e.dma_start(out=w2s[:, j, :], in_=w2[j * 128:(j + 1) * 128, :]),
              writes=[("w2s", j)], dma=True, bulk=True)


def ffn_phase(nc, S, src, dst, w13, w2, lng, lnb, ident_h, ntok, tag, wbuf=None, pre=None):
    ng = ntok // 512
    with ExitStack() as st:
        sb = lambda name, shape, dt: st.enter_context(nc.sbuf_tensor(tag + name, shape, dt))
        ps = lambda name, shape, dt: st.enter_context(nc.psum_tensor(tag + name, shape, dt))
        if wbuf is None:
            w13s, w2s = ffn_alloc_weights(nc, st, tag)
        else:
            w13s, w2s = wbuf
        ident = sb("ident", [128, 128], F32)
        ident16 = sb("ident16", [128, 128], BF16)
        gt = sb("gt", [128, D], F32)
        bt = sb("bt", [128, D], F32)
        mhalf = sb("mhalf", [128, 8], F32)
        NXS = 3 if pre is None else 2
        NXR = 3
        xs = [sb("xs%d" % i, [128, D], BF16) for i in range(NXS)]
        xr = [sb("xr%d" % i, [128, D], F32) for i in range(NXR)]
        xT = [sb("xT%d" % i, [128, 8, 512], BF16) for i in range(2)]
        actT = sb("actT", [128, NF, 512], BF16)
        sg = [sb("sg%d" % i, [128, 512], F32) for i in range(2)]
        stats = [sb("st%d" % i, [128, 2, 6], F32) for i in range(NXR)]
        mv = [sb("mv%d" % i, [128, 2], F32) for i in range(NXR)]
        rstd = [sb("rs%d" % i, [128, 1], F32) for i in range(NXR)]
        nmr = [sb("nm%d" % i, [128, 1], F32) for i in range(NXR)]
        PG = ps("PG", [128, 512], F32)
        PU = ps("PU", [128, 512], F32)
        PY = [ps("PY%d" % i, [128, D], F32) for i in range(2)]
        PT = [ps("PT%d" % i, [128, 8, 128], BF16) for i in range(2)]

        S.add("sp", lambda e: e.dma_start(out=ident[:], in_=ident_h), writes=["ident"], dma=True)
        S.add("sp", lambda e: e.dma_start(out=gt[:], in_=lng.partition_broadcast(128)), writes=["gt"], dma=True)
        S.add("sp", lambda e: e.dma_start(out=bt[:], in_=lnb.partition_broadcast(128)), writes=["bt"], dma=True)
        S.add("pool", lambda e: e.memset(mhalf[:], -0.5), writes=["mhalf"])
        S.add("act", lambda e: e.activation(out=ident16[:], in_=ident[:], func=AF.Copy), reads=["ident"], writes=["ident16"])
        nx = [0]
        if pre is not None:
            g0t = sb("g0t", [128, D], F32)
            b0t = sb("b0t", [128, D], F32)
            g0T = sb("g0T", [128, 8], F32)
            b0T = sb("b0T", [128, 8], F32)
            b0T16 = sb("b0T16", [128, 8], BF16)
            cGU = sb("cGU", [128, 2 * NF], F32)
            S.add("sp", lambda e: e.dma_start(out=g0t[:], in_=pre[0].partition_broadcast(128)), writes=["g0t"], dma=True)
            S.add("sp", lambda e: e.dma_start(out=b0t[:], in_=pre[1].partition_broadcast(128)), writes=["b0t"], dma=True)
            for hh in range(2):
                S.add("sp", lambda e, hh=hh: e.dma_start(
                    out=g0T[:, hh * 4:(hh + 1) * 4], in_=pre[0][:, hh * 512:(hh + 1) * 512].rearrange("o (k p) -> p (o k)", p=128),
                    allow_slow_non_contiguous=True), writes=["g0T"], dma=True)
                S.add("sp", lambda e, hh=hh: e.dma_start(
                    out=b0T[:, hh * 4:(hh + 1) * 4], in_=pre[1][:, hh * 512:(hh + 1) * 512].rearrange("o (k p) -> p (o k)", p=128),
                    allow_slow_non_contiguous=True), writes=["b0T"], dma=True)
            S.add("act", lambda e: e.activation(out=b0T16[:], in_=b0T[:], func=AF.Copy), reads=["b0T"], writes=["b0T16"])
            for j in range(NF):
                for (idx, c0) in ((j, j * 128), (NF + j, DFF + j * 128)):
                    for k in range(8):
                        S.add("pe", lambda e, idx=idx, c0=c0, k=k: e.matmul(
                            out=PG[:, idx:idx + 1], lhsT=w13s[:, k, c0:c0 + 128], rhs=b0T16[:, k:k + 1],
                            start=(k == 0), stop=(k == 7)), reads=[("w13s", c0 // 512), "b0T16"], writes=["PG"])
            S.add("dve", lambda e: e.tensor_copy(out=cGU[:], in_=PG[:, 0:2 * NF]), reads=["PG"], writes=["cGU"])
            for blk in W13_BLOCKS:
                for k in range(8):
                    o_ = w13s[:, k, blk * 512:(blk + 1) * 512]
                    gk = g0T[:, k:k + 1]
                    S.add(("act", "dve"), {
                        "act": lambda e, o_=o_, gk=gk: e.activation(out=o_, in_=o_, func=AF.Copy, scale=gk),
                        "dve": lambda e, o_=o_, gk=gk: e.tensor_scalar(out=o_, in0=o_, scalar1=gk, scalar2=None, op0=ALU.mult)},
                        reads=["g0T"], writes=[("w13s", blk)])

        def stage_a(g):
            b = g % 2
            for t in range(4):
                r0 = g * 512 + t * 128
                xb = nx[0] % NXS
                nx[0] += 1
                pb = t % 2
                S.add("pool", lambda e, r0=r0, xb=xb: e.dma_start(out=xs[xb][:], in_=src[r0:r0 + 128, :]),
                      writes=[("xs", xb)], dma=True)
                for k in range(8):
                    S.add("pe", lambda e, k=k, xb=xb, pb=pb: e.transpose(
                        out=PT[pb][:, k, :], in_=xs[xb][:, k * 128:(k + 1) * 128], identity=ident16[:]),
                        reads=[("xs", xb), "ident16"], writes=[("PT", pb)])
                o_ = xT[b][:, :, t * 128:(t + 1) * 128]
                i_ = PT[pb][:]
                S.add(("act", "dve"), {"act": lambda e, o_=o_, i_=i_: e.activation(out=o_, in_=i_, func=AF.Copy),
                                       "dve": lambda e, o_=o_, i_=i_: e.tensor_copy(out=o_, in_=i_)},
                      reads=[("PT", pb)], writes=[("xT", b, t)])

        def stage_b(g):
            b = g % 2
            for j in range(NF):
                for (P, pk, c0) in ((PG, "PG", j * 128), (PU, "PU", DFF + j * 128)):
                    for k in range(8):
                        S.add("pe", lambda e, P=P, c0=c0, k=k, b=b: e.matmul(
                            out=P[:], lhsT=w13s[:, k, c0:c0 + 128], rhs=xT[b][:, k, :], start=(k == 0), stop=(k == 7)),
                            reads=[("w13s", c0 // 512)] + [("xT", b, t_) for t_ in range(4)], writes=[pk])
                sb_ = j % len(sg)
                if pre is None:
                    S.add("act", lambda e, sb_=sb_: e.activation(out=sg[sb_][:], in_=PG[:], func=AF.Silu),
                          reads=["PG"], writes=[("sg", sb_)])
                    S.add("dve", lambda e, sb_=sb_, j=j: e.tensor_tensor(out=actT[:, j, :], in0=sg[sb_][:], in1=PU[:],
                                                                       op=ALU.mult),
                          reads=[("sg", sb_), "PU"], writes=[("actT", j)])
                else:
                    S.add("act", lambda e, sb_=sb_, j=j: e.activation(out=sg[sb_][:], in_=PG[:], func=AF.Silu, bias=cGU[:, j:j + 1]),
                          reads=["PG", "cGU"], writes=[("sg", sb_)])
                    S.add("dve", lambda e, sb_=sb_, j=j: e.scalar_tensor_tensor(
                        out=actT[:, j, :], in0=PU[:], scalar=cGU[:, NF + j:NF + j + 1], in1=sg[sb_][:], op0=ALU.add, op1=ALU.mult),
                        reads=[("sg", sb_), "PU", "cGU"], writes=[("actT", j)])

        nr = [0]

        def stage_c(g):
            for t in range(4):
                r0 = g * 512 + t * 128
                pyb = t % 2
                b = nr[0] % NXR
                nr[0] += 1
                S.add("sp", lambda e, r0=r0, b=b: e.dma_start(out=xr[b][:], in_=src[r0:r0 + 128, :]),
                      writes=[("xr", b)], dma=True)
                if pre is not None:
                    S.add("dve", lambda e, b=b: e.tensor_tensor(out=xr[b][:], in0=xr[b][:], in1=g0t[:], op=ALU.mult),
                          reads=["g0t"], writes=[("xr", b)])
                    S.add("pool", lambda e, b=b: e.tensor_tensor(out=xr[b][:], in0=xr[b][:], in1=b0t[:], op=ALU.add),
                          reads=["b0t"], writes=[("xr", b)])
                for h in range(2):
                    for j in range(NF):
                        S.add("pe", lambda e, h=h, j=j, t=t, pyb=pyb: e.matmul(
                            out=PY[pyb][:, h * 512:(h + 1) * 512], lhsT=actT[:, j, t * 128:(t + 1) * 128],
                            rhs=w2s[:, j, h * 512:(h + 1) * 512], start=(j == 0), stop=(j == NF - 1)),
                            reads=[("w2s", j), ("actT", j)], writes=[("PY", pyb, h)])
                layer_norm_tile(S, xr[b], ("xr", b), [PY[pyb][:, 0:512], PY[pyb][:, 512:1024]],
                                [("PY", pyb, 0), ("PY", pyb, 1)], 2.0 * ALPHA, 4.0 * EPS, gt, bt,
                                stats[b], mv[b], rstd[b], nmr[b], ("lnst", b), mhalf)
                S.add("sp", lambda e, r0=r0, b=b: e.dma_start(out=dst[r0:r0 + 128, :], in_=xr[b][:]),
                      reads=[("xr", b)], dma=True)

        stage_a(0)
        if wbuf is None:
            ffn_load_weights(S, w13s, w2s, w13, w2)
        stage_b(0)
        if ng > 1:
            stage_a(1)
        for g in range(ng):
            stage_c(g)
            if g + 1 < ng:
                stage_b(g + 1)
            if g + 2 < ng:
                stage_a(g + 2)
        S.barrier()


OFF_QA, OFF_KA, OFF_VA, OFF_QH, OFF_FF, OFF_FB, OFF_IH, OFF_GH = 0, 512, 640, 768, 1280, 1792, 2304, 2816
ATT_SCALE = 64 ** -0.5
NTMP = 3


def host_consts():
    c = {}
    c["ident"] = np.eye(128, dtype=np.float32)
    mr = np.ones((128, 512), np.float32)
    mr[:, ::64] = 0.0
    c["maskreset"] = mr
    s = np.arange(128)[:, None]
    t = np.arange(128)[None, :]
    bias = np.zeros((128, 2, 3, 4, 128), np.float32)
    for g in range(2):
        for cc in range(4):
            slope = 2.0 ** (-8.0 * (4 * g + cc + 1) / 8.0)
            for r in range(3):
                d = (r - 1) * 128 + s - t
                bias[:, g, r, cc, :] = np.where(np.abs(d) <= 128, -slope * np.abs(d), -30000.0)
    c["attbias"] = bias.reshape(128, 2 * 3 * 4 * 128)
    s6 = np.arange(64)[:, None]
    t6 = np.arange(64)[None, :]
    cm = np.zeros((64, 2, 4, 64), np.float32)
    cm[:, 0, :, :] = (s6 <= t6)[:, None, :]
    cm[:, 1, :, :] = (s6 >= t6)[:, None, :]
    c["chunkmask"] = cm.reshape(64, 512)
    return c


def proj_phase(nc, S, X1, w_in, sink, attn_g, lbf, lbb, cst, scr, seqs, tag="p_"):
    with ExitStack() as st:
        sb = lambda name, shape, dt: st.enter_context(nc.sbuf_tensor(tag + name, shape, dt))
        ps = lambda name, shape, dt: st.enter_context(nc.psum_tensor(tag + name, shape, dt))
        tmax = max(seqs)
        wins = sb("wins", [128, 8, INC], BF16)
        ident = sb("ident", [128, 128], F32)
        ident16 = sb("ident16", [128, 128], BF16)
        mreset = sb("mreset", [128, 512], F32)
        attb = sb("attb", [128, 2, 3, 512], BF16)
        gat = sb("gat", [128, 512], F32)
        esink = sb("esink", [128, 8], F32)
        lbp = sb("lbp", [128, 2, 2, 4], F32)
        lb = sb("lb", [128, 2, 4], F32)
        oml = sb("oml", [128, 2, 4], F32)
        noml = sb("noml", [128, 2, 4], F32)
        one = sb("one", [128, 8], F32)
        mhalf = sb("mhalf", [128, 8], F32)
        xs = [sb("xs%d" % i, [128, D], BF16) for i in range(3)]
        xT = [sb("xT%d" % i, [128, 8, 512], BF16) for i in range(2)]
        kT = sb("kT", [128, tmax], BF16)
        vaug = sb("vaug", [128, tmax // 128, 2, 65], BF16)
        qT = [sb("qT%d" % i, [128, 4, 512], BF16) for i in range(2)]
        qsb = [sb("qsb%d" % i, [128, 4, 512], F32) for i in range(2)]
        lnoml = sb("lnoml", [128, 2, 4], F32)
        tmp = [[sb("t%d_%d" % (i, j), [128, 512], F32) for j in range(3)] for i in range(NTMP)]
        tot = [sb("tot%d" % i, [128, 8], F32) for i in range(NTMP)]
        qa = [sb("qa%d" % i, [128, 4, 512], BF16) for i in range(2)]
        ka = [sb("ka%d" % i, [128, 4, 512], BF16) for i in range(2)]
        kdT = [sb("kdT%d" % i, [128, 4, 512], BF16) for i in range(2)]
        sqj = sb("sqj", [128, 512], BF16)
        c1t = [sb("c1t%d" % i, [128, 4, 8], F32) for i in range(2)]
        kdtok = [sb("kdtok%d" % i, [128, 4, 512], BF16) for i in range(2)]
        vtok = [sb("vtok%d" % i, [128, 4, 512], BF16) for i in range(2)]
        gtok = [sb("gtok%d" % i, [128, 512], F32) for i in range(2)]
        pT = [[sb("pT%d_%d" % (i, g_), [128, 3, 512], BF16) for g_ in range(2)] for i in range(2)]
        oat = [sb("oat%d" % i, [128, 512], F32) for i in range(2)]
        atn = [sb("atn%d" % i, [128, 512], BF16) for i in range(2)]
        den = [sb("den%d" % i, [128, 8], F32) for i in range(2)]
        ss = [sb("ss%d" % i, [128, 1], F32) for i in range(2)]
        NB = 6
        PB = [ps("PB%d" % i, [128, 512], F32) for i in range(NB)]
        PO = [ps("PO%d" % i, [128, 4, 65], F32) for i in range(2)]
        rr = {"a": 0, "g": 0, "l": 0}
        pools = {"a": (0, 1), "g": (2, 3), "l": (4, 5)}

        def bank(kind="a"):
            ids = pools[kind]
            i = ids[rr[kind] % len(ids)]
            rr[kind] += 1
            return PB[i], ("PB", i)

        def evac(out, in_, reads, writes, scale=None):
            if scale is None:
                fns = {"act": lambda e: e.activation(out=out, in_=in_, func=AF.Copy),
                       "dve": lambda e: e.tensor_copy(out=out, in_=in_)}
            else:
                fns = {"act": lambda e: e.activation(out=out, in_=in_, func=AF.Copy, scale=scale),
                       "dve": lambda e: e.tensor_scalar(out=out, in0=in_, scalar1=scale, scalar2=None, op0=ALU.mult)}
            S.add(("act", "dve"), fns, reads=reads, writes=writes)

        dma = lambda fn, reads=(), writes=(): S.add("sp", fn, reads=reads, writes=writes, dma=True)
        dma(lambda e: e.dma_start(out=ident[:], in_=cst["ident"]), writes=["ident"])
        dma(lambda e: e.dma_start(out=mreset[:], in_=cst["maskreset"]), writes=["mreset"])
        S.add("pool", lambda e: e.dma_start(out=attb[:].rearrange("p a b c -> p (a b c)"), in_=cst["attbias"]),
              writes=["attb"], dma=True)
        dma(lambda e: e.dma_start(out=gat[:], in_=attn_g.partition_broadcast(128)), writes=["gat"])
        dma(lambda e: e.dma_start(out=esink[:], in_=sink.partition_broadcast(128)), writes=["esink"])
        for d_, lbh in enumerate((lbf, lbb)):
            for slot in range(2):
                dma(lambda e, d_=d_, slot=slot, lbh=lbh: e.dma_start(
                    out=lbp[:, d_, slot, :], in_=lbh[slot:slot + 1, :].rearrange("o (h d) -> d (o h)", d=128),
                    allow_slow_non_contiguous=True), writes=["lbp"])
        for k in range(8):
            S.add("pool", lambda e, k=k: e.dma_start(out=wins[:, k, 512:INC], in_=w_in[k * 128:(k + 1) * 128, 512:INC],
                                                     max_dma_last_dim=4096), writes=[("wins", k, 2)], dma=True, bulk=True)
            for g in range(2):
                S.add("pool", lambda e, k=k, g=g: e.dma_start(
                    out=wins[:, k, 0:512].rearrange("p (c g d) -> p c g d", c=4, g=2)[:, :, g, :],
                    in_=w_in[k * 128:(k + 1) * 128, g * 256:(g + 1) * 256].rearrange("p (c d) -> p c d", c=4)),
                    writes=[("wins", k, g)], dma=True)
        S.add("act", lambda e: e.activation(out=ident16[:], in_=ident[:], func=AF.Copy), reads=["ident"], writes=["ident16"])
        S.add("act", lambda e: e.activation(out=esink[:], in_=esink[:], func=AF.Exp),
              reads=[("wins", k_, p_) for k_ in range(8) for p_ in range(3)], writes=["esink", "wins"])
        S.add("dve", lambda e: e.tensor_tensor(out=lb[:], in0=lbp[:, :, 1, :], in1=lbp[:, :, 0, :], op=ALU.subtract),
              reads=["lbp"], writes=["lb"])
        S.add("act", lambda e: e.activation(out=lb[:], in_=lb[:], func=AF.Exp), writes=["lb"])
        S.add("dve", lambda e: e.tensor_scalar(out=lb[:], in0=lb[:], scalar1=1.0, scalar2=None, op0=ALU.add), writes=["lb"])
        S.add("dve", lambda e: e.reciprocal(out=lb[:], in_=lb[:]), writes=["lb"])
        S.add("dve", lambda e: e.tensor_scalar(out=oml[:], in0=lb[:], scalar1=-1.0, scalar2=1.0, op0=ALU.mult, op1=ALU.add),
              reads=["lb"], writes=["oml"])
        S.add("dve", lambda e: e.tensor_scalar(out=noml[:], in0=lb[:], scalar1=-1.0, scalar2=None, op0=ALU.add),
              reads=["lb"], writes=["noml"])
        S.add("act", lambda e: e.activation(out=lnoml[:], in_=oml[:], func=AF.Ln), reads=["oml"], writes=["lnoml"])
        S.add("pool", lambda e: e.memset(vaug[:], 1.0), writes=[("vaug", i) for i in range(tmax // 128)])
        S.add("pool", lambda e: e.memset(mhalf[:], -0.5), writes=["mhalf"])
        S.add("pool", lambda e: e.memset(one[:], 1.0), writes=["one"])

        def fm_proj(cols_ap, b, kind="a"):
            P, pk = bank(kind)
            for k in range(8):
                S.add("pe", lambda e, P=P, k=k, b=b: e.matmul(out=P[:], lhsT=cols_ap(k), rhs=xT[b][:, k, :],
                                                            start=(k == 0), stop=(k == 7)),
                      reads=["wins"] + [("xT", b, t_) for t_ in range(4)], writes=[pk])
            return P, pk

        def attention_tile(t0, j, ntiles):
            qb = (j // 4) % 2
            tc = slice((j % 4) * 128, (j % 4) * 128 + 128)
            ab = j % 2
            for g in range(2):
                gp = slice(g * 64, (g + 1) * 64)
                rs = [r for r in range(3) if 0 <= j + r - 1 < ntiles]
                for r in rs:
                    jj = j + r - 1
                    P, pk = bank("l")
                    S.add("pe", lambda e, P=P, gp=gp, jj=jj, qb=qb, tc=tc: e.matmul(
                        out=P[:], lhsT=kT[gp, jj * 128:(jj + 1) * 128], rhs=qT[qb][gp, :, tc], start=True, stop=False),
                        reads=[("kT", jj // 4), ("qT", qb)], writes=[pk])
                    S.add("pe", lambda e, P=P, g=g, r=r: e.matmul(
                        out=P[:], lhsT=ident16[:], rhs=attb[:, g, r, :], start=False, stop=True),
                        reads=["ident16", "attb"], writes=[pk])
                    S.add("act", lambda e, P=P, g=g, r=r, ab=ab: e.activation(out=pT[ab][g][:, r, :], in_=P[:], func=AF.Exp),
                          reads=[pk], writes=[("pT", ab, g, r)])
                for c in range(4):
                    for r in rs:
                        jj = j + r - 1
                        S.add("pe", lambda e, g=g, c=c, r=r, jj=jj, rs=rs, ab=ab: e.matmul(
                            out=PO[g][:, c, :], lhsT=pT[ab][g][:, r, c * 128:(c + 1) * 128], rhs=vaug[:, jj, g, :],
                            start=(r == rs[0]), stop=(r == rs[-1])),
                            reads=[("pT", ab, g, r), ("vaug", jj)], writes=[("PO", g)])
                S.add("dve", lambda e, g=g, ab=ab: e.tensor_tensor(out=den[ab][:, g * 4:(g + 1) * 4], in0=PO[g][:, :, 64],
                                                                 in1=esink[:, g * 4:(g + 1) * 4], op=ALU.add),
                      reads=[("PO", g), "esink"], writes=[("den", ab, g)])
                S.add("dve", lambda e, g=g, ab=ab: e.reciprocal(out=den[ab][:, g * 4:(g + 1) * 4], in_=den[ab][:, g * 4:(g + 1) * 4]),
                      writes=[("den", ab, g)])
                S.add("dve", lambda e, g=g, ab=ab: e.tensor_tensor(
                    out=oat[ab][:, g * 256:(g + 1) * 256].rearrange("p (c d) -> p c d", c=4), in0=PO[g][:, :, 0:64],
                    in1=den[ab][:, g * 4:(g + 1) * 4, None].broadcast_to([128, 4, 64]), op=ALU.mult),
                    reads=[("PO", g), ("den", ab, g)], writes=[("oat", ab, g)])
            S.add("act", lambda e, ab=ab: e.activation(out=sqj[:], in_=oat[ab][:], func=AF.Square, accum_out=ss[ab][:]),
                  reads=[("oat", ab, 0), ("oat", ab, 1)], writes=[("ss", ab), "sqj"])
            S.add("dve", lambda e, ab=ab: e.tensor_scalar(out=ss[ab][:], in0=ss[ab][:], scalar1=1.0 / 512, scalar2=EPS,
                                                          op0=ALU.mult, op1=ALU.add), writes=[("ss", ab)])
            S.add("pool", lambda e, ab=ab: e.tensor_tensor(out=ss[ab][:], in0=ss[ab][:], in1=mhalf[:, 0:1], op=ALU.pow),
                  reads=["mhalf"], writes=[("ss", ab)])
            S.add("dve", lambda e, ab=ab: e.scalar_tensor_tensor(out=atn[ab][:], in0=oat[ab][:], scalar=ss[ab][:], in1=gat[:],
                                                               op0=ALU.mult, op1=ALU.mult),
                  reads=[("oat", ab, 0), ("oat", ab, 1), ("ss", ab), "gat"], writes=[("atn", ab)])
            r0 = t0 + j * 128
            dma(lambda e, r0=r0, ab=ab: e.dma_start(out=scr["AT"][r0:r0 + 128, :], in_=atn[ab][:]), reads=[("atn", ab)])

        gsc = 0
        t0 = 0
        nx = 0
        for T in seqs:
            nsc = T // 512
            ntiles = T // 128
            for sc in range(nsc):
                b = sc % 2
                for t in range(4):
                    r0 = t0 + sc * 512 + t * 128
                    xb = nx % 3
                    nx += 1
                    S.add("pool", lambda e, r0=r0, xb=xb: e.dma_start(out=xs[xb][:], in_=X1[r0:r0 + 128, :]),
                          writes=[("xs", xb)], dma=True)
                    P, pk = bank()
                    P16 = P[:].bitcast(BF16)
                    Pv = P16.rearrange("p (a b) -> p a b", a=8)
                    for k in range(8):
                        S.add("pe", lambda e, Pv=Pv, k=k, xb=xb: e.transpose(
                            out=Pv[:, k, :], in_=xs[xb][:, k * 128:(k + 1) * 128], identity=ident16[:]),
                            reads=[("xs", xb), "ident16"], writes=[pk])
                    evac(xT[b][:, :, t * 128:(t + 1) * 128], Pv, [pk], [("xT", b, t)])
                for c in range(4):
                    P, pk = bank()
                    for k in range(8):
                        wq = wins[:, k, c * 128:(c + 1) * 128]
                        S.add("pe", lambda e, P=P, k=k, b=b, wq=wq: e.matmul(
                            out=P[:], lhsT=wq, rhs=xT[b][:, k, :], start=(k == 0), stop=(k == 7)),
                            reads=["wins"] + [("xT", b, t_) for t_ in range(4)], writes=[pk])
                    evac(qT[b][:, c, :], P[:], [pk], [("qT", b)], scale=ATT_SCALE)
                P, pk = fm_proj(lambda k: wins[:, k, OFF_KA:OFF_KA + 128], b)
                evac(kT[:, sc * 512:(sc + 1) * 512], P[:], [pk], [("kT", sc)])
                for t in range(4):
                    jt = sc * 4 + t
                    for (c0, n, kind) in ((OFF_VA, 128, "va"), (OFF_IH, 512, "ih"), (OFF_GH, 512, "gh")):
                        P, pk = bank()
                        for k in range(8):
                            S.add("pe", lambda e, P=P, k=k, c0=c0, n=n, t=t, b=b: e.matmul(
                                out=P[:, 0:n], lhsT=xT[b][:, k, t * 128:(t + 1) * 128], rhs=wins[:, k, c0:c0 + n],
                                start=(k == 0), stop=(k == 7)),
                                reads=["wins", ("xT", b, t)], writes=[pk])
                        if kind == "va":
                            evac(vaug[:, jt, :, 0:64], P[:, 0:128].rearrange("p (g d) -> p g d", g=2), [pk], [("vaug", jt)])
                        elif kind == "ih":
                            evac(vtok[b][:, t, :], P[:], [pk], [("vtok", b, t)])
                        else:
                            evac(gtok[t % 2][:], P[:], [pk], [("gtok", t % 2)])
                            rg = t0 + sc * 512 + t * 128
                            dma(lambda e, rg=rg, t=t: e.dma_start(out=scr["G"][rg:rg + 128, :], in_=gtok[t % 2][:]),
                                reads=[("gtok", t % 2)])
                rows = slice(t0 + sc * 512, t0 + (sc + 1) * 512)
                dma(lambda e, rows=rows, b=b: e.dma_start(out=scr["V"][rows, :].rearrange("(t p) c -> p t c", p=128), in_=vtok[b][:]),
                    reads=[("vtok", b, t_) for t_ in range(4)])
                for h in range(4):
                    P, pk = fm_proj(lambda k, h=h: wins[:, k, OFF_QH + h * 128:OFF_QH + (h + 1) * 128], b)
                    evac(qsb[b][:, h, :], P[:], [pk], [("qsb", b, h)])
                for d_ in range(2):
                    foff = OFF_FF if d_ == 0 else OFF_FB
                    for h in range(4):
                        tb = (d_ * 4 + h) % NTMP
                        t1, t2, t3 = tmp[tb]
                        tk = lambda i, tb=tb: ("tmp", tb, i)
                        P, pk = fm_proj(lambda k, h=h, foff=foff: wins[:, k, foff + h * 128:foff + (h + 1) * 128], b, "g")
                        lbc, lnc = lb[:, d_, h:h + 1], lnoml[:, d_, h:h + 1]
                        cpos = 63 if d_ == 0 else 0
                        t1v = t1[:].rearrange("p (c t) -> p c t", c=8)
                        t2v = t2[:].rearrange("p (c t) -> p c t", c=8)
                        t3v = t3[:].rearrange("p (c t) -> p c t", c=8)
                        S.add("act", lambda e, P=P, t1=t1: e.activation(out=t1[:], in_=P[:], func=AF.Exp, scale=-1.0),
                              reads=[pk], writes=[tk(1)])
                        S.add("act", lambda e, t1=t1, t2=t2: e.activation(out=t2[:], in_=t1[:], func=AF.Ln, bias=one[:, 0:1], scale=1.0),
                              reads=[tk(1), "one"], writes=[tk(2)])
                        S.add("act", lambda e, t1=t1, t3=t3, lbc=lbc: e.activation(out=t3[:], in_=t1[:], func=AF.Ln, bias=one[:, 0:1], scale=lbc),
                              reads=[tk(1), "one", "lb"], writes=[tk(3)])
                        S.add("dve", lambda e, t2=t2, t3=t3: e.tensor_tensor(out=t3[:], in0=t3[:], in1=t2[:], op=ALU.subtract),
                              reads=[tk(2)], writes=[tk(3)])
                        S.add("dve", lambda e, P=P, t2=t2: e.tensor_tensor(out=t2[:], in0=P[:], in1=t2[:], op=ALU.add),
                              reads=[pk], writes=[tk(2)])
                        S.add("dve", lambda e, t1=t1, t3=t3: e.tensor_tensor_scan(
                            out=t1[:], data0=mreset[:], data1=t3[:], initial=0.0, op0=ALU.mult, op1=ALU.add),
                            reads=[tk(3), "mreset"], writes=[tk(1)])
                        if d_ == 1:
                            S.add("dve", lambda e, t1v=t1v, tb=tb: e.tensor_copy(out=tot[tb][:], in_=t1v[:, :, 63]),
                                  reads=[tk(1)], writes=[("tot", tb)])
                            S.add("dve", lambda e, t1=t1, t3=t3: e.scalar_tensor_tensor(
                                out=t1[:], in0=t1[:], scalar=-1.0, in1=t3[:], op0=ALU.mult, op1=ALU.add),
                                reads=[tk(3)], writes=[tk(1)])
                            S.add("dve", lambda e, t1v=t1v, tb=tb: e.tensor_tensor(
                                out=t1v, in0=t1v, in1=tot[tb][:, :, None].broadcast_to([128, 8, 64]), op=ALU.add),
                                reads=[("tot", tb)], writes=[tk(1)])
                        S.add("act", lambda e, t1=t1, t3=t3: e.activation(out=t3[:], in_=t1[:], func=AF.Exp),
                              reads=[tk(1)], writes=[tk(3)])
                        S.add("dve", lambda e, d_=d_, h=h, t3=t3, b=b: e.tensor_tensor(out=qa[d_][:, h, :], in0=qsb[b][:, h, :], in1=t3[:],
                                                                                     op=ALU.mult),
                              reads=[("qsb", b, h), tk(3)], writes=[("qa", d_, h)])
                        S.add("dve", lambda e, d_=d_, h=h, t3v=t3v, cpos=cpos: e.tensor_copy(out=c1t[d_][:, h, :], in_=t3v[:, :, cpos]),
                              reads=[tk(3)], writes=[("c1t", d_, h)])
                        S.add("dve", lambda e, t1=t1, t2=t2: e.tensor_tensor(out=t2[:], in0=t2[:], in1=t1[:], op=ALU.add),
                              reads=[tk(1)], writes=[tk(2)])
                        S.add("act", lambda e, d_=d_, h=h, t2=t2, lnc=lnc: e.activation(
                            out=ka[d_][:, h, :], in_=t2[:], func=AF.Exp, bias=lnc, scale=-1.0),
                            reads=[tk(2), "lnoml"], writes=[("ka", d_, h)])
                        S.add("dve", lambda e, t1v=t1v, t2v=t2v, cpos=cpos: e.tensor_tensor(
                            out=t2v, in0=t2v, in1=t1v[:, :, cpos:cpos + 1].broadcast_to([128, 8, 64]), op=ALU.subtract),
                            reads=[tk(1)], writes=[tk(2)])
                        S.add("act", lambda e, d_=d_, h=h, t2=t2, lnc=lnc: e.activation(
                            out=kdT[d_][:, h, :], in_=t2[:], func=AF.Exp, bias=lnc, scale=-1.0),
                            reads=[tk(2), "lnoml"], writes=[("kdT", d_, h)])
                    for t in range(4):
                        P, pk = bank()
                        P16 = P[:].bitcast(BF16)[:, 0:512]
                        Pv = P16.rearrange("p (a b) -> p a b", a=4)
                        for h in range(4):
                            S.add("pe", lambda e, Pv=Pv, h=h, t=t, d_=d_: e.transpose(
                                out=Pv[:, h, :], in_=kdT[d_][:, h, t * 128:(t + 1) * 128], identity=ident16[:]),
                                reads=[("kdT", d_, h), "ident16"], writes=[pk])
                        evac(kdtok[d_][:, t, :], P16, [pk], [("kdtok", d_, t)])
                    dma(lambda e, d_=d_, gsc=gsc: e.dma_start(out=scr["QA"][d_][gsc], in_=qa[d_][:]),
                        reads=[("qa", d_, h_) for h_ in range(4)])
                    dma(lambda e, d_=d_, gsc=gsc: e.dma_start(out=scr["KA"][d_][gsc], in_=ka[d_][:]),
                        reads=[("ka", d_, h_) for h_ in range(4)])
                    dma(lambda e, d_=d_, gsc=gsc: e.dma_start(out=scr["C1"][d_][gsc], in_=c1t[d_][:]),
                        reads=[("c1t", d_, h_) for h_ in range(4)])
                    dma(lambda e, d_=d_, rows=rows: e.dma_start(
                        out=scr["KD"][d_][rows, :].rearrange("(t p) c -> p t c", p=128), in_=kdtok[d_][:]),
                        reads=[("kdtok", d_, t_) for t_ in range(4)])
                jlo = max(sc * 4 - 1, 0)
                jhi = sc * 4 + 3 if sc < nsc - 1 else sc * 4 + 4
                for j in range(jlo, jhi):
                    attention_tile(t0, j, ntiles)
                gsc += 1
            t0 += T
        S.barrier()


def scan_phase(nc, S, cst, scr, seqs, tag="s_", prefetch=None):
    with ExitStack() as st:
        sb = lambda name, shape, dt: st.enter_context(nc.sbuf_tensor(tag + name, shape, dt))
        ps = lambda name, shape, dt: st.enter_context(nc.psum_tensor(tag + name, shape, dt))
        cmask = sb("cmask", [64, 2, 256], F32)
        qa_s = [[sb("qa%d_%d" % (d_, i), [128, 4, 512], BF16) for i in range(2)] for d_ in range(2)]
        ka_s = [[sb("ka%d_%d" % (d_, i), [128, 4, 512], BF16) for i in range(2)] for d_ in range(2)]
        kd_s = [[sb("kd%d_%d" % (d_, i), [64, 8, 512], BF16) for i in range(2)] for d_ in range(2)]
        v_s = [[sb("v%d_%d" % (d_, i), [64, 8, 512], BF16) for i in range(2)] for d_ in range(2)]
        c1_s = [[sb("c1%d_%d" % (d_, i), [128, 4, 8], F32) for i in range(2)] for d_ in range(2)]
        S32 = [sb("S32_%d" % d_, [128, 512], F32) for d_ in range(2)]
        S16 = [sb("S16_%d" % d_, [128, 512], BF16) for d_ in range(2)]
        at16 = [sb("at16_%d" % d_, [64, 256], BF16) for d_ in range(2)]
        osb = [[sb("osb%d_%d" % (d_, i), [64, 512], F32) for i in range(2)] for d_ in range(2)]
        PA = [ps("PA%d" % d_, [128, 512], F32) for d_ in range(2)]
        PO = [ps("PO%d" % d_, [128, 512], F32) for d_ in range(2)]
        PU = [ps("PU%d" % d_, [128, 512], F32) for d_ in range(2)]
        PO2 = [ps("PO2_%d" % d_, [128, 512], F32) for d_ in range(2)]
        dma = lambda fn, reads=(), writes=(): S.add("sp", fn, reads=reads, writes=writes, dma=True)
        dma(lambda e: e.dma_start(out=cmask[:].rearrange("p a b -> p (a b)"), in_=cst["chunkmask"]), writes=["cmask"])
        if prefetch is not None:
            prefetch()
        OUT = (scr["OF"], scr["OB"])
        gsc0 = 0
        t0 = 0
        nld = [0, 0]
        for T in seqs:
            nsc = T // 512
            for d_ in range(2):
                S.add("pool", lambda e, d_=d_: e.memset(S32[d_][:], 0.0), writes=[("S32", d_)])
                S.add("pool", lambda e, d_=d_: e.memset(S16[d_][:], 0.0), writes=[("S16", d_)])

            def load(d_, sc):
                bi = nld[d_] % 2
                nld[d_] += 1
                gsc = gsc0 + sc
                rows = slice(t0 + sc * 512, t0 + (sc + 1) * 512)
                k = ("in", d_, bi)
                dma(lambda e: e.dma_start(out=qa_s[d_][bi][:], in_=scr["QA"][d_][gsc]), writes=[k])
                dma(lambda e: e.dma_start(out=ka_s[d_][bi][:], in_=scr["KA"][d_][gsc]), writes=[k])
                dma(lambda e: e.dma_start(out=c1_s[d_][bi][:], in_=scr["C1"][d_][gsc]), writes=[k])
                dma(lambda e: e.dma_start(out=kd_s[d_][bi][:], in_=scr["KD"][d_][rows, :].rearrange("(c p) f -> p c f", p=64)),
                    writes=[k])
                dma(lambda e: e.dma_start(out=v_s[d_][bi][:], in_=scr["V"][rows, :].rearrange("(c p) f -> p c f", p=64)),
                    writes=[k])
                return bi

            pend = [load(0, 0), load(1, nsc - 1)]
            for i in range(nsc):
                cur = pend
                scs = (i, nsc - 1 - i)
                if i + 1 < nsc:
                    pend = [load(0, i + 1), load(1, nsc - 2 - i)]
                for cc in range(8):
                    for d_ in range(2):
                        bi = cur[d_]
                        c = cc if d_ == 0 else 7 - cc
                        cs = slice(c * 64, (c + 1) * 64)
                        ink = ("in", d_, bi)
                        ob = cc % 2
                        for h in range(4):
                            S.add("pe", lambda e, d_=d_, bi=bi, h=h, cs=cs: e.matmul(
                                out=PA[d_][0:64, h * 64:(h + 1) * 64], lhsT=ka_s[d_][bi][:, h, cs], rhs=qa_s[d_][bi][:, h, cs],
                                start=True, stop=True), reads=[ink], writes=[("PA", d_)])
                        S.add("dve", lambda e, d_=d_: e.tensor_tensor(out=at16[d_][:], in0=PA[d_][0:64, 0:256], in1=cmask[:, d_, :],
                                                                    op=ALU.mult),
                              reads=[("PA", d_), "cmask"], writes=[("at16", d_)])
                        for h in range(4):
                            hs = slice(h * 128, (h + 1) * 128)
                            S.add("pe", lambda e, d_=d_, bi=bi, h=h, hs=hs, cs=cs: e.matmul(
                                out=PO2[d_][0:64, hs], lhsT=qa_s[d_][bi][:, h, cs], rhs=S16[d_][:, hs],
                                start=True, stop=True), reads=[ink, ("S16", d_)], writes=[("PO2", d_)])
                        for h in range(4):
                            hs = slice(h * 128, (h + 1) * 128)
                            S.add("pe", lambda e, d_=d_, bi=bi, h=h, hs=hs, c=c: e.matmul(
                                out=PO[d_][0:64, hs], lhsT=at16[d_][:, h * 64:(h + 1) * 64], rhs=v_s[d_][bi][:, c, hs],
                                start=True, stop=True), reads=[ink, ("at16", d_)], writes=[("PO", d_)])
                        for h in range(4):
                            hs = slice(h * 128, (h + 1) * 128)
                            S.add("pe", lambda e, d_=d_, bi=bi, hs=hs, c=c: e.matmul(
                                out=PU[d_][:, hs], lhsT=kd_s[d_][bi][:, c, hs], rhs=v_s[d_][bi][:, c, hs],
                                start=True, stop=True), reads=[ink], writes=[("PU", d_)])
                        S.add("act", lambda e, d_=d_, ob=ob: e.activation(out=osb[d_][ob][:], in_=PO2[d_][0:64, :], func=AF.Copy),
                              reads=[("PO2", d_)], writes=[("osb", d_, ob)])
                        S.add("dve", lambda e, d_=d_, ob=ob: e.tensor_tensor(out=osb[d_][ob][:], in0=osb[d_][ob][:], in1=PO[d_][0:64, :],
                                                                             op=ALU.add),
                              reads=[("PO", d_)], writes=[("osb", d_, ob)])
                        r0 = t0 + scs[d_] * 512 + c * 64
                        sc_mine = scs[d_]
                        first = (sc_mine < nsc // 2) if d_ == 0 else (sc_mine >= nsc // 2)
                        rk = ("OFrow", r0)
                        if first:
                            dma(lambda e, ob=ob, d_=d_, r0=r0: e.dma_start(out=scr["OF"][r0:r0 + 64, :], in_=osb[d_][ob][:]),
                                reads=[("osb", d_, ob)], writes=[rk])
                        else:
                            S.add("pool", lambda e, ob=ob, d_=d_, r0=r0: e.dma_start(
                                out=scr["OF"][r0:r0 + 64, :], in_=osb[d_][ob][:], accum_op=ALU.add),
                                reads=[("osb", d_, ob)], writes=[rk], dma=True)
                        for h in range(4):
                            hs = slice(h * 128, (h + 1) * 128)
                            S.add("dve", lambda e, d_=d_, bi=bi, h=h, hs=hs, c=c: e.scalar_tensor_tensor(
                                out=S32[d_][:, hs], in0=S32[d_][:, hs], scalar=c1_s[d_][bi][:, h, c:c + 1], in1=PU[d_][:, hs],
                                op0=ALU.mult, op1=ALU.add), reads=[ink, ("PU", d_)], writes=[("S32", d_)])
                        S.add("act", lambda e, d_=d_: e.activation(out=S16[d_][:], in_=S32[d_][:], func=AF.Copy),
                              reads=[("S32", d_)], writes=[("S16", d_)])
            gsc0 += nsc
            t0 += T
        S.barrier()


def combine_phase(nc, S, X1, X2, w_out, hg_g, lng, lnb, cst, scr, ntok, tag="c_", prefetch=None):
    NB = 3
    with ExitStack() as st:
        sb = lambda name, shape, dt: st.enter_context(nc.sbuf_tensor(tag + name, shape, dt))
        ps = lambda name, shape, dt: st.enter_context(nc.psum_tensor(tag + name, shape, dt))
        wouts = sb("wouts", [128, 8, D], BF16)
        ident = sb("ident", [128, 128], F32)
        ident16 = sb("ident16", [128, 128], BF16)
        gt = sb("gt", [128, D], F32)
        bt = sb("bt", [128, D], F32)
        hgT = sb("hgT", [128, 4], F32)
        xr = [sb("xr%d" % i, [128, D], F32) for i in range(NB)]
        of = [sb("of%d" % i, [128, 512], F32) for i in range(NB)]
        gg = [sb("gg%d" % i, [128, 512], F32) for i in range(NB)]
        at = [sb("at%d" % i, [128, 512], BF16) for i in range(NB)]
        mixh = [sb("mixh%d" % i, [128, 512], BF16) for i in range(2)]
        mixT = [sb("mixT%d" % i, [128, 8, 128], BF16) for i in range(2)]
        junk = sb("junk", [128, 128], BF16)
        mhalf = sb("mhalf", [128, 8], F32)
        ssq = [sb("ssq%d" % i, [128, 4], F32) for i in range(NB)]
        stats = [sb("st%d" % i, [128, 2, 6], F32) for i in range(NB)]
        mv = [sb("mv%d" % i, [128, 2], F32) for i in range(NB)]
        rstd = [sb("rs%d" % i, [128, 1], F32) for i in range(NB)]
        nmr = [sb("nm%d" % i, [128, 1], F32) for i in range(NB)]
        PT = [ps("PT%d" % i, [128, 8, 128], BF16) for i in range(2)]
        PY = [ps("PY%d" % i, [128, D], F32) for i in range(NB)]
        dma = lambda fn, reads=(), writes=(): S.add("sp", fn, reads=reads, writes=writes, dma=True)
        dma(lambda e: e.dma_start(out=ident[:], in_=cst["ident"]), writes=["ident"])
        dma(lambda e: e.dma_start(out=hgT[:], in_=hg_g.rearrange("o (k p) -> p (o k)", p=128),
                                  allow_slow_non_contiguous=True), writes=["hgT"])
        S.add("pool", lambda e: e.memset(mhalf[:], -0.5), writes=["mhalf"])
        load_w_bf16(S, wouts, w_out, 8, "wouts")
        if prefetch is not None:
            prefetch()
        S.add("act", lambda e: e.activation(out=ident16[:], in_=ident[:], func=AF.Copy), reads=["ident"], writes=["ident16"])
        for k in range(4, 8):
            S.add("act", lambda e, k=k: e.activation(out=wouts[:, k, :], in_=wouts[:, k, :], func=AF.Copy, scale=hgT[:, k - 4:k - 3]),
                  reads=["hgT"] + [("wouts", k_) for k_ in range(8)], writes=["wouts"])
        for it in range(ntok // 128):
            b = it % NB
            b2 = it % 2
            rows = slice(it * 128, (it + 1) * 128)
            dma(lambda e, b=b, rows=rows: e.dma_start(out=of[b][:], in_=scr["OF"][rows, :]), writes=[("of", b)])
            dma(lambda e, b=b, rows=rows: e.dma_start(out=gg[b][:], in_=scr["G"][rows, :]), writes=[("gg", b)])
            dma(lambda e, b=b, rows=rows: e.dma_start(out=at[b][:], in_=scr["AT"][rows, :]), writes=[("at", b)])
            dma(lambda e, b=b, rows=rows: e.dma_start(out=xr[b][:], in_=X1[rows, :]), writes=[("xr", b)])
            for h in range(4):
                hs = slice(h * 128, (h + 1) * 128)
                S.add("act", lambda e, b=b, h=h, hs=hs: e.activation(out=junk[:], in_=of[b][:, hs], func=AF.Square,
                                                                    accum_out=ssq[b][:, h:h + 1]),
                      reads=[("of", b)], writes=[("ssq", b, h), "junk"])
            S.add("dve", lambda e, b=b: e.tensor_scalar(out=ssq[b][:], in0=ssq[b][:], scalar1=1.0 / 128, scalar2=EPS,
                                                        op0=ALU.mult, op1=ALU.add),
                  reads=[("ssq", b, h) for h in range(4)], writes=[("ssq", b)])
            S.add("pool", lambda e, b=b: e.tensor_tensor(out=ssq[b][:], in0=ssq[b][:], in1=mhalf[:, 0:4], op=ALU.pow),
                  reads=["mhalf"], writes=[("ssq", b)])
            S.add("act", lambda e, b=b: e.activation(out=gg[b][:], in_=gg[b][:], func=AF.Silu), writes=[("gg", b)])
            for h in range(4):
                hs = slice(h * 128, (h + 1) * 128)
                S.add("dve", lambda e, b=b, b2=b2, h=h, hs=hs: e.scalar_tensor_tensor(
                    out=mixh[b2][:, hs], in0=of[b][:, hs], scalar=ssq[b][:, h:h + 1], in1=gg[b][:, hs], op0=ALU.mult, op1=ALU.mult),
                    reads=[("ssq", b), ("of", b), ("gg", b)], writes=[("mixh", b2)])
            for k in range(8):
                if k < 4:
                    S.add("pe", lambda e, k=k, b=b, b2=b2: e.transpose(
                        out=PT[b2][:, k, :], in_=at[b][:, k * 128:(k + 1) * 128], identity=ident16[:]),
                        reads=[("at", b), "ident16"], writes=[("PT", b2)])
                else:
                    S.add("pe", lambda e, k=k, b2=b2: e.transpose(
                        out=PT[b2][:, k, :], in_=mixh[b2][:, (k - 4) * 128:(k - 3) * 128], identity=ident16[:]),
                        reads=[("mixh", b2), "ident16"], writes=[("PT", b2)])
            S.add("act", lambda e, b2=b2: e.activation(out=mixT[b2][:], in_=PT[b2][:], func=AF.Copy),
                  reads=[("PT", b2)], writes=[("mixT", b2)])
            for h in range(2):
                for k in range(8):
                    S.add("pe", lambda e, h=h, k=k, b=b, b2=b2: e.matmul(
                        out=PY[b][:, h * 512:(h + 1) * 512], lhsT=mixT[b2][:, k, :], rhs=wouts[:, k, h * 512:(h + 1) * 512],
                        start=(k == 0), stop=(k == 7)),
                        reads=["wouts", ("mixT", b2)], writes=[("PY", b, h)])
            layer_norm_tile(S, xr[b], ("xr", b), [PY[b][:, 0:512], PY[b][:, 512:1024]], [("PY", b, 0), ("PY", b, 1)],
                            ALPHA, EPS, gt, bt, stats[b], mv[b], rstd[b], nmr[b], ("lnst", b), mhalf, affine=False)
            dma(lambda e, b=b, rows=rows: e.dma_start(out=X2[rows, :], in_=xr[b][:]), reads=[("xr", b)])
        S.barrier()


def build(seqs, debug=False, upto=5):
    ntok = sum(seqs)
    nsct = ntok // 512
    nc = bass.Bass("TRN2", target_bir_lowering=False)
    dr = lambda name, shape, dt, kind="ExternalInput": nc.dram_tensor(name, shape, dt, kind=kind).ap()
    x = dr("x", [ntok, D], F32)
    ln_g = dr("ln_g", [3, D], F32)
    ln_b = dr("ln_b", [3, D], F32)
    w13 = dr("ffn_w13", [2, D, 2 * DFF], F32)
    w2 = dr("ffn_w2", [2, DFF, D], F32)
    w_in = dr("w_in", [D, INC], F32)
    w_out = dr("w_out", [D, D], F32)
    sink = dr("attn_sink", [1, 8], F32)
    attn_g = dr("attn_norm_g", [1, 512], F32)
    lbf = dr("hg_lb_fwd", [2, 512], F32)
    lbb = dr("hg_lb_bwd", [2, 512], F32)
    hg_g = dr("hg_norm_g", [1, 512], F32)
    cst = {"ident": dr("ident", [128, 128], F32), "maskreset": dr("maskreset", [128, 512], F32),
           "attbias": dr("attbias", [128, 3072], F32), "chunkmask": dr("chunkmask", [64, 512], F32)}
    y = dr("y", [ntok, D], F32, "ExternalOutput")
    sk = "ExternalOutput" if debug else "Internal"
    scr = {
        "QA": [dr("QA%d" % d_, [nsct, 128, 4, 512], BF16, sk) for d_ in range(2)],
        "KA": [dr("KA%d" % d_, [nsct, 128, 4, 512], BF16, sk) for d_ in range(2)],
        "C1": [dr("C1%d" % d_, [nsct, 128, 4, 8], F32, sk) for d_ in range(2)],
        "KD": [dr("KD%d" % d_, [ntok, 512], BF16, sk) for d_ in range(2)],
        "V": dr("Vs", [ntok, 512], BF16, sk), "G": dr("Gs", [ntok, 512], F32, sk),
        "AT": dr("ATs", [ntok, 512], BF16, sk), "OF": dr("OFs", [ntok, 512], F32, sk), "OB": dr("OBs", [ntok, 512], F32, sk),
    }
    X1 = dr("X1s", [ntok, D], F32, sk)
    X2 = dr("X2s", [ntok, D], F32, sk)
    S = Sched()
    with ExitStack() as st:
        sems = {}
        for c in list(ENGS) + ["dma%d" % i for i in range(S.n_dma_sems)] + ["wdma%d" % i for i in range(S.N_WSEMS)] + ["sdma%d" % i for i in range(S.N_SSEMS)]:
            sems[c] = st.enter_context(nc.semaphore("s_" + c))
        if upto >= 1:
            ffn_phase(nc, S, x, X1, w13[0], w2[0], ln_g[0:1, :], ln_b[0:1, :], cst["ident"], ntok, "a_")
        if upto >= 2:
            proj_phase(nc, S, X1, w_in, sink, attn_g, lbf, lbb, cst, scr, seqs)
        if upto >= 3:
            wb = [st.enter_context(nc.sbuf_tensor("b_w13s", [128, 8, 2 * DFF], BF16)), None]
            scan_phase(nc, S, cst, scr, seqs,
                       prefetch=lambda: ffn_load_weights(S, wb[0], wb[1], w13[1], w2[1], parts=(0,)))
        if upto >= 4:
            wb[1] = st.enter_context(nc.sbuf_tensor("b_w2s", [128, NF, D], BF16))
            combine_phase(nc, S, X1, X2, w_out, hg_g, ln_g[1:2, :], ln_b[1:2, :], cst, scr, ntok,
                          prefetch=lambda: ffn_load_weights(S, wb[0], wb[1], w13[1], w2[1], parts=(1,)))
        if upto >= 5:
            ffn_phase(nc, S, X2, y, w13[1], w2[1], ln_g[2:3, :], ln_b[2:3, :], cst["ident"], ntok, "b_", wbuf=wb,
                      pre=(ln_g[1:2, :], ln_b[1:2, :]))
        S.barrier()
        with nc.Block() as block:
            S.emit(block, sems)
    nc._sched_phase_times = S.phase_times
    return nc


def core_inputs(inputs, xc):
    m = {"x": np.ascontiguousarray(xc, dtype=np.float32)}
    m["ln_g"] = np.ascontiguousarray(inputs["ln_g"][0])
    m["ln_b"] = np.ascontiguousarray(inputs["ln_b"][0])
    m["ffn_w13"] = np.ascontiguousarray(inputs["ffn_w13"][0])
    m["ffn_w2"] = np.ascontiguousarray(inputs["ffn_w2"][0])
    m["w_in"] = np.ascontiguousarray(inputs["w_in"][0])
    m["w_out"] = np.ascontiguousarray(inputs["w_out"][0])
    m["attn_sink"] = np.ascontiguousarray(inputs["attn_sink"]).reshape(1, 8)
    m["attn_norm_g"] = np.ascontiguousarray(inputs["attn_norm_g"]).reshape(1, 512)
    m["hg_lb_fwd"] = np.ascontiguousarray(inputs["hg_lb_fwd"])
    m["hg_lb_bwd"] = np.ascontiguousarray(inputs["hg_lb_bwd"])
    m["hg_norm_g"] = np.ascontiguousarray(inputs["hg_norm_g"]).reshape(1, 512)
    m.update(host_consts())
    return m


def kernel(**inputs):
    inputs = {k: np.asarray(v) for k, v in inputs.items()}
    xp, xsm = inputs["x_prompt"], inputs["x_sample"]
    n = 8
    B, T, _ = xp.shape
    Bs, Ts, _ = xsm.shape
    per_p, per_s = B // n, Bs // n
    seqs = [T] * per_p + [Ts] * per_s
    nc = build(seqs)
    in_maps = []
    for c in range(n):
        parts = [xp[c * per_p + i] for i in range(per_p)] + [xsm[c * per_s + i] for i in range(per_s)]
        in_maps.append(core_inputs(inputs, np.concatenate(parts, axis=0)))
    res = run_bass_kernel_spmd(nc, in_maps, core_ids=list(range(n)))
    yp = np.empty_like(xp, dtype=np.float32)
    ys = np.empty_like(xsm, dtype=np.float32)
    for c in range(n):
        yc = res.results[c]["y"]
        o = 0
        for i in range(per_p):
            yp[c * per_p + i] = yc[o:o + T]
            o += T
        for i in range(per_s):
            ys[c * per_s + i] = yc[o:o + Ts]
            o += Ts
    return (yp, ys)
```

```python
import numpy as np
from contextlib import ExitStack
import concourse.bass as bass
import concourse.mybir as mybir
from concourse.bass_utils import run_bass_kernel_spmd

F32 = mybir.dt.float32
BF16 = mybir.dt.bfloat16
AF = mybir.ActivationFunctionType
ALU = mybir.AluOpType

D = 1024
DFF = 2816
NF = DFF // 128
INC = 3328
ALPHA = 2.0 ** 0.25
EPS = 1e-5
ENGS = ("pe", "act", "dve", "pool", "sp")


import heapq


class _Op:
    __slots__ = ("fn", "eng", "waits", "ctr", "is_dma", "preds", "succs", "cost", "lat", "idx", "npred", "ready", "ev", "opts", "done", "bulk")


class _Rec:
    def __getattr__(self, name):
        def f(*a, **kw):
            object.__setattr__(self, "iname", name)
            object.__setattr__(self, "kw", kw)
            return self
        return f


_F32 = mybir.dt.float32
SYNC_LAT = 0.25
PE_COL = 1.0 / 2400.0


def _estimate(eng, fn, dma):
    r = _Rec()
    fn(r)
    kw = r.kw
    name = r.iname
    if dma:
        nb = 0
        for k in ("out", "in_"):
            try:
                nb = max(nb, int(kw[k].nbytes))
            except Exception:
                pass
        return (0.7 if eng == "pool" else 0.45), 2.5 + nb / 120e3
    n = 1
    for k in ("out", "in_", "in0", "in1", "data0", "data1", "rhs"):
        a = kw.get(k)
        if a is not None and hasattr(a, "shape"):
            m = 1
            for s in list(a.shape)[1:]:
                m *= int(s)
            if k == "rhs" or name not in ("matmul", "transpose"):
                n = max(n, m)
    if name == "matmul":
        passes = 4 if kw["rhs"].dtype == _F32 else 1
        return max(0.07, 0.03 + passes * n * PE_COL * _PE_SLOW[0]), 0.0
    if name == "transpose":
        passes = 4 if kw["in_"].dtype == _F32 else 1
        return max(0.07, 0.03 + passes * 128 * PE_COL * _PE_SLOW[0]), 0.0
    if eng == "act":
        return 0.12 + n * 0.001, 0.0
    if eng == "dve":
        if name == "reciprocal":
            return 0.1 + n * 0.0065, 0.0
        return 0.1 + n * 0.00115, 0.0
    return 0.2 + n * 0.002, 0.0


_PE_SLOW = [1.0]


class Sched:
    N_WSEMS = 10
    N_SSEMS = 8

    def __init__(self, n_dma_sems=32):
        self.wdma_rr = 0
        self.sdma_rr = 0
        self.bulk_hist = []
        self.ops = {e: [] for e in ENGS}
        self.clock = {e: {} for e in ENGS}
        self.count = {}
        self.last_w = {}
        self.readers = {}
        self.n_dma_sems = n_dma_sems
        self.dma_rr = 0
        self.dma_last = {}
        self.pending = []
        self.phase_times = []

    def _need(self, eng, deps):
        clk = self.clock[eng]
        best = {}
        for (c, v, s) in deps:
            if clk.get(c, 0) >= v:
                continue
            if c not in best or best[c][0] < v:
                best[c] = (v, s)
        items = list(best.items())
        waits = []
        for c, (v, s) in items:
            implied = False
            for c2, (v2, s2) in items:
                if c2 != c and s2.get(c, 0) >= v:
                    implied = True
                    break
            if not implied:
                waits.append((c, v))
        for c, (v, s) in items:
            for cc, vv in s.items():
                if clk.get(cc, 0) < vv:
                    clk[cc] = vv
            if clk.get(c, 0) < v:
                clk[c] = v
        return waits

    def add(self, eng, fn, reads=(), writes=(), dma=False, bulk=False):
        op = _Op()
        op.is_dma = dma
        op.bulk = bulk
        if isinstance(eng, tuple):
            op.opts = {e: (fn[e],) + _estimate(e, fn[e], dma) for e in eng}
            op.eng = None
            op.fn = None
        else:
            c_, l_ = _estimate(eng, fn, dma)
            if bulk:
                l_ = 2.5 + (l_ - 2.5) * 1.5
            op.opts = {eng: (fn, c_, l_)}
            op.eng = eng
            op.fn = fn
        preds = {}
        for k in reads:
            w = self.last_w.get(k)
            if w is not None:
                preds[id(w)] = w
        for k in writes:
            w = self.last_w.get(k)
            if w is not None:
                preds[id(w)] = w
            for r_ in self.readers.get(k, ()):
                preds[id(r_)] = r_
        if bulk:
            if len(self.bulk_hist) >= self.N_WSEMS:
                w = self.bulk_hist[-self.N_WSEMS]
                preds[id(w)] = w
            self.bulk_hist.append(op)
        preds.pop(id(op), None)
        op.preds = list(preds.values())
        op.succs = []
        for k in reads:
            self.readers.setdefault(k, []).append(op)
        for k in writes:
            self.last_w[k] = op
            self.readers[k] = []
        self.pending.append(op)
        return op

    def _list_schedule(self, ops):
        for i, op in enumerate(ops):
            op.idx = i
            op.npred = len(op.preds)
            op.ready = 0.0
            op.done = False
        for op in ops:
            for p in op.preds:
                p.succs.append(op)
        fut = {e: [] for e in ENGS}
        rdy = {e: [] for e in ENGS}
        free = {e: 0.0 for e in ENGS}
        for op in ops:
            if op.npred == 0:
                for e in op.opts:
                    heapq.heappush(fut[e], (0.0, op.idx))
        order = []
        nleft = len(ops)
        while nleft:
            bt, be = None, None
            for e in ENGS:
                while rdy[e] and ops[rdy[e][0]].done:
                    heapq.heappop(rdy[e])
                while fut[e] and ops[fut[e][0][1]].done:
                    heapq.heappop(fut[e])
                if rdy[e]:
                    t = free[e]
                elif fut[e]:
                    t = max(free[e], fut[e][0][0])
                else:
                    continue
                if bt is None or t < bt:
                    bt, be = t, e
            assert be is not None, "dependency cycle"
            e, t = be, bt
            while fut[e] and fut[e][0][0] <= t:
                _, i = heapq.heappop(fut[e])
                if not ops[i].done:
                    heapq.heappush(rdy[e], i)
            op = ops[heapq.heappop(rdy[e])]
            op.done = True
            op.eng = e
            op.fn, op.cost, op.lat = op.opts[e]
            if op.is_dma:
                free[e] = t + op.cost
                fin = free[e] + op.lat
            else:
                fin = t + op.cost
                free[e] = fin
            order.append(op)
            nleft -= 1
            for s in op.succs:
                r_ = fin + (SYNC_LAT if (s.eng != e or op.is_dma) else 0.0)
                if r_ > s.ready:
                    s.ready = r_
                s.npred -= 1
                if s.npred == 0:
                    for e2 in s.opts:
                        heapq.heappush(fut[e2], (s.ready, s.idx))
        self.sim_time = max(free.values()) if order else 0.0
        return order

    def _commit(self, op):
        eng = op.eng
        deps = [p.ev for p in op.preds]
        if eng == "pe":
            deps = [d for d in deps if d[0] != "pe"]
        if op.is_dma:
            if op.bulk:
                ctr = "wdma%d" % self.wdma_rr
                self.wdma_rr = (self.wdma_rr + 1) % self.N_WSEMS
            elif eng == "pool":
                ctr = "sdma%d" % self.sdma_rr
                self.sdma_rr = (self.sdma_rr + 1) % self.N_SSEMS
            else:
                ctr = "dma%d" % self.dma_rr
                self.dma_rr = (self.dma_rr + 1) % self.n_dma_sems
            prev = self.dma_last.get(ctr)
            if prev is not None:
                deps.append(prev)
            step = 16
        else:
            ctr = eng
            step = 1
        op.waits = self._need(eng, deps)
        val = self.count.get(ctr, 0) + step
        self.count[ctr] = val
        op.ctr = ctr
        snap = dict(self.clock[eng])
        snap[ctr] = val
        op.ev = (ctr, val, snap)
        if op.is_dma:
            self.dma_last[ctr] = op.ev
        self.ops[eng].append(op)

    def flush(self):
        ops, self.pending = self.pending, []
        if not ops:
            return
        for op in self._list_schedule(ops):
            self._commit(op)
            op.preds = None
            op.succs = None
        busy = {}
        for op in ops:
            busy[op.eng] = busy.get(op.eng, 0.0) + op.cost
        self.phase_times.append((len(ops), round(self.sim_time, 1), {k: round(v, 1) for k, v in busy.items()}))

    def barrier(self):
        self.flush()
        allc = [(c, v, {}) for c, v in self.count.items()]
        for eng in ENGS:
            op = _Op()
            op.fn = None
            op.is_dma = False
            op.ctr = None
            op.waits = self._need(eng, allc)
            self.ops[eng].append(op)
        self.last_w = {}
        self.readers = {}
        self.bulk_hist = []

    def emit(self, block, sems):
        engobj = {"pe": "tensor", "act": "scalar", "dve": "vector", "pool": "gpsimd", "sp": "sync"}

        def mk(engname):
            def body(e):
                for op in self.ops[engname]:
                    for (c, v) in op.waits:
                        e.wait_ge(sems[c], v)
                    if op.fn is None:
                        continue
                    ins = op.fn(e)
                    ins.then_inc(sems[op.ctr], 16 if op.is_dma else 1)
            return body

        for engname in ENGS:
            getattr(block, engobj[engname])(mk(engname))


def load_w_bf16(S, dst, src_rows, nk, key):
    for k in range(nk):
        S.add("pool", lambda e, k=k: e.dma_start(out=dst[:, k, :], in_=src_rows[k * 128:(k + 1) * 128, :],
                                                 max_dma_last_dim=4096),
              writes=[(key, k)], dma=True, bulk=True)


def layer_norm_tile(S, xr, xkey, py_halves, pykeys, res_scale, eps, gt, bt, stats, mv, rstd, nmr, skey, mhalf, affine=True):
    for h in range(2):
        sl = slice(h * 512, (h + 1) * 512)
        S.add("dve", lambda e, h=h, sl=sl: e.scalar_tensor_tensor(out=xr[:, sl], in0=xr[:, sl], scalar=res_scale,
                                                                 in1=py_halves[h], op0=ALU.mult, op1=ALU.add),
              reads=[pykeys[h]], writes=[xkey])
        S.add("dve", lambda e, h=h, sl=sl: e.bn_stats(out=stats[:, h, :], in_=xr[:, sl]),
              reads=[xkey], writes=[skey])
    S.add("dve", lambda e: e.bn_aggr(out=mv[:], in_=stats[:].rearrange("p a b -> p (a b)")), reads=[skey], writes=[skey])
    S.add("dve", lambda e: e.tensor_scalar(out=rstd[:], in0=mv[:, 1:2], scalar1=eps, scalar2=None,
                                           op0=ALU.add), reads=[skey], writes=[skey])
    S.add("pool", lambda e: e.tensor_tensor(out=rstd[:], in0=rstd[:], in1=mhalf[:, 0:1], op=ALU.pow),
          reads=[skey, "mhalf"], writes=[skey])
    S.add("dve", lambda e: e.scalar_tensor_tensor(out=nmr[:], in0=mv[:, 0:1], scalar=-1.0, in1=rstd[:],
                                                  op0=ALU.mult, op1=ALU.mult), reads=[skey], writes=[skey])
    S.add("act", lambda e: e.activation(out=xr[:], in_=xr[:], func=AF.Identity, bias=nmr[:], scale=rstd[:]),
          reads=[skey], writes=[xkey])
    if affine:
        S.add("dve", lambda e: e.tensor_tensor(out=xr[:], in0=xr[:], in1=gt[:], op=ALU.mult), reads=["gt"], writes=[xkey])
        S.add("pool", lambda e: e.tensor_tensor(out=xr[:], in0=xr[:], in1=bt[:], op=ALU.add), reads=["bt"], writes=[xkey])


W13_BLOCKS = (0, 5, 6, 1, 7, 2, 8, 3, 9, 4, 10)


def ffn_alloc_weights(nc, st, tag):
    w13s = st.enter_context(nc.sbuf_tensor(tag + "w13s", [128, 8, 2 * DFF], BF16))
    w2s = st.enter_context(nc.sbuf_tensor(tag + "w2s", [128, NF, D], BF16))
    return w13s, w2s


def ffn_load_weights(S, w13s, w2s, w13, w2, parts=(0, 1)):
    for i, blk in enumerate(W13_BLOCKS if 0 in parts else ()):
        c0 = blk * 512
        S.add("pool", lambda e, c0=c0: e.dma_start(out=w13s[:, :, c0:c0 + 512],
                                                  in_=w13[:, c0:c0 + 512].rearrange("(k p) c -> p k c", p=128)),
              writes=[("w13s", blk)], dma=True, bulk=True)
    for j in range(NF if 1 in parts else 0):
        S.add("pool", lambda e, j=j: e.dma_start(out=w2s[:, j, :], in_=w2[j * 128:(j + 1) * 128, :]),
              writes=[("w2s", j)], dma=True, bulk=True)


def ffn_phase(nc, S, src, dst, w13, w2, lng, lnb, ident_h, ntok, tag, wbuf=None, pre=None):
    ng = ntok // 512
    with ExitStack() as st:
        sb = lambda name, shape, dt: st.enter_context(nc.sbuf_tensor(tag + name, shape, dt))
        ps = lambda name, shape, dt: st.enter_context(nc.psum_tensor(tag + name, shape, dt))
        if wbuf is None:
            w13s, w2s = ffn_alloc_weights(nc, st, tag)
        else:
            w13s, w2s = wbuf
        ident = sb("ident", [128, 128], F32)
        ident16 = sb("ident16", [128, 128], BF16)
        gt = sb("gt", [128, D], F32)
        bt = sb("bt", [128, D], F32)
        mhalf = sb("mhalf", [128, 8], F32)
        NXS = 3 if pre is None else 2
        NXR = 3
        xs = [sb("xs%d" % i, [128, D], BF16) for i in range(NXS)]
        xr = [sb("xr%d" % i, [128, D], F32) for i in range(NXR)]
        xT = [sb("xT%d" % i, [128, 8, 512], BF16) for i in range(2)]
        actT = sb("actT", [128, NF, 512], BF16)
        sg = [sb("sg%d" % i, [128, 512], F32) for i in range(2)]
        stats = [sb("st%d" % i, [128, 2, 6], F32) for i in range(NXR)]
        mv = [sb("mv%d" % i, [128, 2], F32) for i in range(NXR)]
        rstd = [sb("rs%d" % i, [128, 1], F32) for i in range(NXR)]
        nmr = [sb("nm%d" % i, [128, 1], F32) for i in range(NXR)]
        PG = ps("PG", [128, 512], F32)
        PU = ps("PU", [128, 512], F32)
        PY = [ps("PY%d" % i, [128, D], F32) for i in range(2)]
        PT = [ps("PT%d" % i, [128, 8, 128], BF16) for i in range(2)]

        S.add("sp", lambda e: e.dma_start(out=ident[:], in_=ident_h), writes=["ident"], dma=True)
        S.add("sp", lambda e: e.dma_start(out=gt[:], in_=lng.partition_broadcast(128)), writes=["gt"], dma=True)
        S.add("sp", lambda e: e.dma_start(out=bt[:], in_=lnb.partition_broadcast(128)), writes=["bt"], dma=True)
        S.add("pool", lambda e: e.memset(mhalf[:], -0.5), writes=["mhalf"])
        S.add("act", lambda e: e.activation(out=ident16[:], in_=ident[:], func=AF.Copy), reads=["ident"], writes=["ident16"])
        nx = [0]
        if pre is not None:
            g0t = sb("g0t", [128, D], F32)
            b0t = sb("b0t", [128, D], F32)
            g0T = sb("g0T", [128, 8], F32)
            b0T = sb("b0T", [128, 8], F32)
            b0T16 = sb("b0T16", [128, 8], BF16)
            cGU = sb("cGU", [128, 2 * NF], F32)
            S.add("sp", lambda e: e.dma_start(out=g0t[:], in_=pre[0].partition_broadcast(128)), writes=["g0t"], dma=True)
            S.add("sp", lambda e: e.dma_start(out=b0t[:], in_=pre[1].partition_broadcast(128)), writes=["b0t"], dma=True)
            for hh in range(2):
                S.add("sp", lambda e, hh=hh: e.dma_start(
                    out=g0T[:, hh * 4:(hh + 1) * 4], in_=pre[0][:, hh * 512:(hh + 1) * 512].rearrange("o (k p) -> p (o k)", p=128),
                    allow_slow_non_contiguous=True), writes=["g0T"], dma=True)
                S.add("sp", lambda e, hh=hh: e.dma_start(
                    out=b0T[:, hh * 4:(hh + 1) * 4], in_=pre[1][:, hh * 512:(hh + 1) * 512].rearrange("o (k p) -> p (o k)", p=128),
                    allow_slow_non_contiguous=True), writes=["b0T"], dma=True)
            S.add("act", lambda e: e.activation(out=b0T16[:], in_=b0T[:], func=AF.Copy), reads=["b0T"], writes=["b0T16"])
            for j in range(NF):
                for (idx, c0) in ((j, j * 128), (NF + j, DFF + j * 128)):
                    for k in range(8):
                        S.add("pe", lambda e, idx=idx, c0=c0, k=k: e.matmul(
                            out=PG[:, idx:idx + 1], lhsT=w13s[:, k, c0:c0 + 128], rhs=b0T16[:, k:k + 1],
                            start=(k == 0), stop=(k == 7)), reads=[("w13s", c0 // 512), "b0T16"], writes=["PG"])
            S.add("dve", lambda e: e.tensor_copy(out=cGU[:], in_=PG[:, 0:2 * NF]), reads=["PG"], writes=["cGU"])
            for blk in W13_BLOCKS:
                for k in range(8):
                    o_ = w13s[:, k, blk * 512:(blk + 1) * 512]
                    gk = g0T[:, k:k + 1]
                    S.add(("act", "dve"), {
                        "act": lambda e, o_=o_, gk=gk: e.activation(out=o_, in_=o_, func=AF.Copy, scale=gk),
                        "dve": lambda e, o_=o_, gk=gk: e.tensor_scalar(out=o_, in0=o_, scalar1=gk, scalar2=None, op0=ALU.mult)},
                        reads=["g0T"], writes=[("w13s", blk)])

        def stage_a(g):
            b = g % 2
            for t in range(4):
                r0 = g * 512 + t * 128
                xb = nx[0] % NXS
                nx[0] += 1
                pb = t % 2
                S.add("pool", lambda e, r0=r0, xb=xb: e.dma_start(out=xs[xb][:], in_=src[r0:r0 + 128, :]),
                      writes=[("xs", xb)], dma=True)
                for k in range(8):
                    S.add("pe", lambda e, k=k, xb=xb, pb=pb: e.transpose(
                        out=PT[pb][:, k, :], in_=xs[xb][:, k * 128:(k + 1) * 128], identity=ident16[:]),
                        reads=[("xs", xb), "ident16"], writes=[("PT", pb)])
                o_ = xT[b][:, :, t * 128:(t + 1) * 128]
                i_ = PT[pb][:]
                S.add(("act", "dve"), {"act": lambda e, o_=o_, i_=i_: e.activation(out=o_, in_=i_, func=AF.Copy),
                                       "dve": lambda e, o_=o_, i_=i_: e.tensor_copy(out=o_, in_=i_)},
                      reads=[("PT", pb)], writes=[("xT", b, t)])

        def stage_b(g):
            b = g % 2
            for j in range(NF):
                for (P, pk, c0) in ((PG, "PG", j * 128), (PU, "PU", DFF + j * 128)):
                    for k in range(8):
                        S.add("pe", lambda e, P=P, c0=c0, k=k, b=b: e.matmul(
                            out=P[:], lhsT=w13s[:, k, c0:c0 + 128], rhs=xT[b][:, k, :], start=(k == 0), stop=(k == 7)),
                            reads=[("w13s", c0 // 512)] + [("xT", b, t_) for t_ in range(4)], writes=[pk])
                sb_ = j % len(sg)
                if pre is None:
                    S.add("act", lambda e, sb_=sb_: e.activation(out=sg[sb_][:], in_=PG[:], func=AF.Silu),
                          reads=["PG"], writes=[("sg", sb_)])
                    S.add("dve", lambda e, sb_=sb_, j=j: e.tensor_tensor(out=actT[:, j, :], in0=sg[sb_][:], in1=PU[:],
                                                                       op=ALU.mult),
                          reads=[("sg", sb_), "PU"], writes=[("actT", j)])
                else:
                    S.add("act", lambda e, sb_=sb_, j=j: e.activation(out=sg[sb_][:], in_=PG[:], func=AF.Silu, bias=cGU[:, j:j + 1]),
                          reads=["PG", "cGU"], writes=[("sg", sb_)])
                    S.add("dve", lambda e, sb_=sb_, j=j: e.scalar_tensor_tensor(
                        out=actT[:, j, :], in0=PU[:], scalar=cGU[:, NF + j:NF + j + 1], in1=sg[sb_][:], op0=ALU.add, op1=ALU.mult),
                        reads=[("sg", sb_), "PU", "cGU"], writes=[("actT", j)])

        nr = [0]

        def stage_c(g):
            for t in range(4):
                r0 = g * 512 + t * 128
                pyb = t % 2
                b = nr[0] % NXR
                nr[0] += 1
                S.add("sp", lambda e, r0=r0, b=b: e.dma_start(out=xr[b][:], in_=src[r0:r0 + 128, :]),
                      writes=[("xr", b)], dma=True)
                if pre is not None:
                    S.add("dve", lambda e, b=b: e.tensor_tensor(out=xr[b][:], in0=xr[b][:], in1=g0t[:], op=ALU.mult),
                          reads=["g0t"], writes=[("xr", b)])
                    S.add("pool", lambda e, b=b: e.tensor_tensor(out=xr[b][:], in0=xr[b][:], in1=b0t[:], op=ALU.add),
                          reads=["b0t"], writes=[("xr", b)])
                for h in range(2):
                    for j in range(NF):
                        S.add("pe", lambda e, h=h, j=j, t=t, pyb=pyb: e.matmul(
                            out=PY[pyb][:, h * 512:(h + 1) * 512], lhsT=actT[:, j, t * 128:(t + 1) * 128],
                            rhs=w2s[:, j, h * 512:(h + 1) * 512], start=(j == 0), stop=(j == NF - 1)),
                            reads=[("w2s", j), ("actT", j)], writes=[("PY", pyb, h)])
                layer_norm_tile(S, xr[b], ("xr", b), [PY[pyb][:, 0:512], PY[pyb][:, 512:1024]],
                                [("PY", pyb, 0), ("PY", pyb, 1)], 2.0 * ALPHA, 4.0 * EPS, gt, bt,
                                stats[b], mv[b], rstd[b], nmr[b], ("lnst", b), mhalf)
                S.add("sp", lambda e, r0=r0, b=b: e.dma_start(out=dst[r0:r0 + 128, :], in_=xr[b][:]),
                      reads=[("xr", b)], dma=True)

        stage_a(0)
        if wbuf is None:
            ffn_load_weights(S, w13s, w2s, w13, w2)
        stage_b(0)
        if ng > 1:
            stage_a(1)
        for g in range(ng):
            stage_c(g)
            if g + 1 < ng:
                stage_b(g + 1)
            if g + 2 < ng:
                stage_a(g + 2)
        S.barrier()


OFF_QA, OFF_KA, OFF_VA, OFF_QH, OFF_FF, OFF_FB, OFF_IH, OFF_GH = 0, 512, 640, 768, 1280, 1792, 2304, 2816
ATT_SCALE = 64 ** -0.5
NTMP = 3


def host_consts():
    c = {}
    c["ident"] = np.eye(128, dtype=np.float32)
    mr = np.ones((128, 512), np.float32)
    mr[:, ::64] = 0.0
    c["maskreset"] = mr
    s = np.arange(128)[:, None]
    t = np.arange(128)[None, :]
    bias = np.zeros((128, 2, 3, 4, 128), np.float32)
    for g in range(2):
        for cc in range(4):
            slope = 2.0 ** (-8.0 * (4 * g + cc + 1) / 8.0)
            for r in range(3):
                d = (r - 1) * 128 + s - t
                bias[:, g, r, cc, :] = np.where(np.abs(d) <= 128, -slope * np.abs(d), -30000.0)
    c["attbias"] = bias.reshape(128, 2 * 3 * 4 * 128)
    s6 = np.arange(64)[:, None]
    t6 = np.arange(64)[None, :]
    cm = np.zeros((64, 2, 4, 64), np.float32)
    cm[:, 0, :, :] = (s6 <= t6)[:, None, :]
    cm[:, 1, :, :] = (s6 >= t6)[:, None, :]
    c["chunkmask"] = cm.reshape(64, 512)
    return c


def proj_phase(nc, S, X1, w_in, sink, attn_g, lbf, lbb, cst, scr, seqs, tag="p_"):
    with ExitStack() as st:
        sb = lambda name, shape, dt: st.enter_context(nc.sbuf_tensor(tag + name, shape, dt))
        ps = lambda name, shape, dt: st.enter_context(nc.psum_tensor(tag + name, shape, dt))
        tmax = max(seqs)
        wins = sb("wins", [128, 8, INC], BF16)
        ident = sb("ident", [128, 128], F32)
        ident16 = sb("ident16", [128, 128], BF16)
        mreset = sb("mreset", [128, 512], F32)
        attb = sb("attb", [128, 2, 3, 512], BF16)
        gat = sb("gat", [128, 512], F32)
        esink = sb("esink", [128, 8], F32)
        lbp = sb("lbp", [128, 2, 2, 4], F32)
        lb = sb("lb", [128, 2, 4], F32)
        oml = sb("oml", [128, 2, 4], F32)
        noml = sb("noml", [128, 2, 4], F32)
        one = sb("one", [128, 8], F32)
        mhalf = sb("mhalf", [128, 8], F32)
        xs = [sb("xs%d" % i, [128, D], BF16) for i in range(3)]
        xT = [sb("xT%d" % i, [128, 8, 512], BF16) for i in range(2)]
        kT = sb("kT", [128, tmax], BF16)
        vaug = sb("vaug", [128, tmax // 128, 2, 65], BF16)
        qT = [sb("qT%d" % i, [128, 4, 512], BF16) for i in range(2)]
        qsb = [sb("qsb%d" % i, [128, 4, 512], F32) for i in range(2)]
        lnoml = sb("lnoml", [128, 2, 4], F32)
        tmp = [[sb("t%d_%d" % (i, j), [128, 512], F32) for j in range(3)] for i in range(NTMP)]
        tot = [sb("tot%d" % i, [128, 8], F32) for i in range(NTMP)]
        qa = [sb("qa%d" % i, [128, 4, 512], BF16) for i in range(2)]
        ka = [sb("ka%d" % i, [128, 4, 512], BF16) for i in range(2)]
        kdT = [sb("kdT%d" % i, [128, 4, 512], BF16) for i in range(2)]
        sqj = sb("sqj", [128, 512], BF16)
        c1t = [sb("c1t%d" % i, [128, 4, 8], F32) for i in range(2)]
        kdtok = [sb("kdtok%d" % i, [128, 4, 512], BF16) for i in range(2)]
        vtok = [sb("vtok%d" % i, [128, 4, 512], BF16) for i in range(2)]
        gtok = [sb("gtok%d" % i, [128, 512], F32) for i in range(2)]
        pT = [[sb("pT%d_%d" % (i, g_), [128, 3, 512], BF16) for g_ in range(2)] for i in range(2)]
        oat = [sb("oat%d" % i, [128, 512], F32) for i in range(2)]
        atn = [sb("atn%d" % i, [128, 512], BF16) for i in range(2)]
        den = [sb("den%d" % i, [128, 8], F32) for i in range(2)]
        ss = [sb("ss%d" % i, [128, 1], F32) for i in range(2)]
        NB = 6
        PB = [ps("PB%d" % i, [128, 512], F32) for i in range(NB)]
        PO = [ps("PO%d" % i, [128, 4, 65], F32) for i in range(2)]
        rr = {"a": 0, "g": 0, "l": 0}
        pools = {"a": (0, 1), "g": (2, 3), "l": (4, 5)}

        def bank(kind="a"):
            ids = pools[kind]
            i = ids[rr[kind] % len(ids)]
            rr[kind] += 1
            return PB[i], ("PB", i)

        def evac(out, in_, reads, writes, scale=None):
            if scale is None:
                fns = {"act": lambda e: e.activation(out=out, in_=in_, func=AF.Copy),
                       "dve": lambda e: e.tensor_copy(out=out, in_=in_)}
            else:
                fns = {"act": lambda e: e.activation(out=out, in_=in_, func=AF.Copy, scale=scale),
                       "dve": lambda e: e.tensor_scalar(out=out, in0=in_, scalar1=scale, scalar2=None, op0=ALU.mult)}
            S.add(("act", "dve"), fns, reads=reads, writes=writes)

        dma = lambda fn, reads=(), writes=(): S.add("sp", fn, reads=reads, writes=writes, dma=True)
        dma(lambda e: e.dma_start(out=ident[:], in_=cst["ident"]), writes=["ident"])
        dma(lambda e: e.dma_start(out=mreset[:], in_=cst["maskreset"]), writes=["mreset"])
        S.add("pool", lambda e: e.dma_start(out=attb[:].rearrange("p a b c -> p (a b c)"), in_=cst["attbias"]),
              writes=["attb"], dma=True)
        dma(lambda e: e.dma_start(out=gat[:], in_=attn_g.partition_broadcast(128)), writes=["gat"])
        dma(lambda e: e.dma_start(out=esink[:], in_=sink.partition_broadcast(128)), writes=["esink"])
        for d_, lbh in enumerate((lbf, lbb)):
            for slot in range(2):
                dma(lambda e, d_=d_, slot=slot, lbh=lbh: e.dma_start(
                    out=lbp[:, d_, slot, :], in_=lbh[slot:slot + 1, :].rearrange("o (h d) -> d (o h)", d=128),
                    allow_slow_non_contiguous=True), writes=["lbp"])
        for k in range(8):
            S.add("pool", lambda e, k=k: e.dma_start(out=wins[:, k, 512:INC], in_=w_in[k * 128:(k + 1) * 128, 512:INC],
                                                     max_dma_last_dim=4096), writes=[("wins", k, 2)], dma=True, bulk=True)
            for g in range(2):
                S.add("pool", lambda e, k=k, g=g: e.dma_start(
                    out=wins[:, k, 0:512].rearrange("p (c g d) -> p c g d", c=4, g=2)[:, :, g, :],
                    in_=w_in[k * 128:(k + 1) * 128, g * 256:(g + 1) * 256].rearrange("p (c d) -> p c d", c=4)),
                    writes=[("wins", k, g)], dma=True)
        S.add("act", lambda e: e.activation(out=ident16[:], in_=ident[:], func=AF.Copy), reads=["ident"], writes=["ident16"])
        S.add("act", lambda e: e.activation(out=esink[:], in_=esink[:], func=AF.Exp),
              reads=[("wins", k_, p_) for k_ in range(8) for p_ in range(3)], writes=["esink", "wins"])
        S.add("dve", lambda e: e.tensor_tensor(out=lb[:], in0=lbp[:, :, 1, :], in1=lbp[:, :, 0, :], op=ALU.subtract),
              reads=["lbp"], writes=["lb"])
        S.add("act", lambda e: e.activation(out=lb[:], in_=lb[:], func=AF.Exp), writes=["lb"])
        S.add("dve", lambda e: e.tensor_scalar(out=lb[:], in0=lb[:], scalar1=1.0, scalar2=None, op0=ALU.add), writes=["lb"])
        S.add("dve", lambda e: e.reciprocal(out=lb[:], in_=lb[:]), writes=["lb"])
        S.add("dve", lambda e: e.tensor_scalar(out=oml[:], in0=lb[:], scalar1=-1.0, scalar2=1.0, op0=ALU.mult, op1=ALU.add),
              reads=["lb"], writes=["oml"])
        S.add("dve", lambda e: e.tensor_scalar(out=noml[:], in0=lb[:], scalar1=-1.0, scalar2=None, op0=ALU.add),
              reads=["lb"], writes=["noml"])
        S.add("act", lambda e: e.activation(out=lnoml[:], in_=oml[:], func=AF.Ln), reads=["oml"], writes=["lnoml"])
        S.add("pool", lambda e: e.memset(vaug[:], 1.0), writes=[("vaug", i) for i in range(tmax // 128)])
        S.add("pool", lambda e: e.memset(mhalf[:], -0.5), writes=["mhalf"])
        S.add("pool", lambda e: e.memset(one[:], 1.0), writes=["one"])

        def fm_proj(cols_ap, b, kind="a"):
            P, pk = bank(kind)
            for k in range(8):
                S.add("pe", lambda e, P=P, k=k, b=b: e.matmul(out=P[:], lhsT=cols_ap(k), rhs=xT[b][:, k, :],
                                                            start=(k == 0), stop=(k == 7)),
                      reads=["wins"] + [("xT", b, t_) for t_ in range(4)], writes=[pk])
            return P, pk

        def attention_tile(t0, j, ntiles):
            qb = (j // 4) % 2
            tc = slice((j % 4) * 128, (j % 4) * 128 + 128)
            ab = j % 2
            for g in range(2):
                gp = slice(g * 64, (g + 1) * 64)
                rs = [r for r in range(3) if 0 <= j + r - 1 < ntiles]
                for r in rs:
                    jj = j + r - 1
                    P, pk = bank("l")
                    S.add("pe", lambda e, P=P, gp=gp, jj=jj, qb=qb, tc=tc: e.matmul(
                        out=P[:], lhsT=kT[gp, jj * 128:(jj + 1) * 128], rhs=qT[qb][gp, :, tc], start=True, stop=False),
                        reads=[("kT", jj // 4), ("qT", qb)], writes=[pk])
                    S.add("pe", lambda e, P=P, g=g, r=r: e.matmul(
                        out=P[:], lhsT=ident16[:], rhs=attb[:, g, r, :], start=False, stop=True),
                        reads=["ident16", "attb"], writes=[pk])
                    S.add("act", lambda e, P=P, g=g, r=r, ab=ab: e.activation(out=pT[ab][g][:, r, :], in_=P[:], func=AF.Exp),
                          reads=[pk], writes=[("pT", ab, g, r)])
                for c in range(4):
                    for r in rs:
                        jj = j + r - 1
                        S.add("pe", lambda e, g=g, c=c, r=r, jj=jj, rs=rs, ab=ab: e.matmul(
                            out=PO[g][:, c, :], lhsT=pT[ab][g][:, r, c * 128:(c + 1) * 128], rhs=vaug[:, jj, g, :],
                            start=(r == rs[0]), stop=(r == rs[-1])),
                            reads=[("pT", ab, g, r), ("vaug", jj)], writes=[("PO", g)])
                S.add("dve", lambda e, g=g, ab=ab: e.tensor_tensor(out=den[ab][:, g * 4:(g + 1) * 4], in0=PO[g][:, :, 64],
                                                                 in1=esink[:, g * 4:(g + 1) * 4], op=ALU.add),
                      reads=[("PO", g), "esink"], writes=[("den", ab, g)])
                S.add("dve", lambda e, g=g, ab=ab: e.reciprocal(out=den[ab][:, g * 4:(g + 1) * 4], in_=den[ab][:, g * 4:(g + 1) * 4]),
                      writes=[("den", ab, g)])
                S.add("dve", lambda e, g=g, ab=ab: e.tensor_tensor(
                    out=oat[ab][:, g * 256:(g + 1) * 256].rearrange("p (c d) -> p c d", c=4), in0=PO[g][:, :, 0:64],
                    in1=den[ab][:, g * 4:(g + 1) * 4, None].broadcast_to([128, 4, 64]), op=ALU.mult),
                    reads=[("PO", g), ("den", ab, g)], writes=[("oat", ab, g)])
            S.add("act", lambda e, ab=ab: e.activation(out=sqj[:], in_=oat[ab][:], func=AF.Square, accum_out=ss[ab][:]),
                  reads=[("oat", ab, 0), ("oat", ab, 1)], writes=[("ss", ab), "sqj"])
            S.add("dve", lambda e, ab=ab: e.tensor_scalar(out=ss[ab][:], in0=ss[ab][:], scalar1=1.0 / 512, scalar2=EPS,
                                                          op0=ALU.mult, op1=ALU.add), writes=[("ss", ab)])
            S.add("pool", lambda e, ab=ab: e.tensor_tensor(out=ss[ab][:], in0=ss[ab][:], in1=mhalf[:, 0:1], op=ALU.pow),
                  reads=["mhalf"], writes=[("ss", ab)])
            S.add("dve", lambda e, ab=ab: e.scalar_tensor_tensor(out=atn[ab][:], in0=oat[ab][:], scalar=ss[ab][:], in1=gat[:],
                                                               op0=ALU.mult, op1=ALU.mult),
                  reads=[("oat", ab, 0), ("oat", ab, 1), ("ss", ab), "gat"], writes=[("atn", ab)])
            r0 = t0 + j * 128
            dma(lambda e, r0=r0, ab=ab: e.dma_start(out=scr["AT"][r0:r0 + 128, :], in_=atn[ab][:]), reads=[("atn", ab)])

        gsc = 0
        t0 = 0
        nx = 0
        for T in seqs:
            nsc = T // 512
            ntiles = T // 128
            for sc in range(nsc):
                b = sc % 2
                for t in range(4):
                    r0 = t0 + sc * 512 + t * 128
                    xb = nx % 3
                    nx += 1
                    S.add("pool", lambda e, r0=r0, xb=xb: e.dma_start(out=xs[xb][:], in_=X1[r0:r0 + 128, :]),
                          writes=[("xs", xb)], dma=True)
                    P, pk = bank()
                    P16 = P[:].bitcast(BF16)
                    Pv = P16.rearrange("p (a b) -> p a b", a=8)
                    for k in range(8):
                        S.add("pe", lambda e, Pv=Pv, k=k, xb=xb: e.transpose(
                            out=Pv[:, k, :], in_=xs[xb][:, k * 128:(k + 1) * 128], identity=ident16[:]),
                            reads=[("xs", xb), "ident16"], writes=[pk])
                    evac(xT[b][:, :, t * 128:(t + 1) * 128], Pv, [pk], [("xT", b, t)])
                for c in range(4):
                    P, pk = bank()
                    for k in range(8):
                        wq = wins[:, k, c * 128:(c + 1) * 128]
                        S.add("pe", lambda e, P=P, k=k, b=b, wq=wq: e.matmul(
                            out=P[:], lhsT=wq, rhs=xT[b][:, k, :], start=(k == 0), stop=(k == 7)),
                            reads=["wins"] + [("xT", b, t_) for t_ in range(4)], writes=[pk])
                    evac(qT[b][:, c, :], P[:], [pk], [("qT", b)], scale=ATT_SCALE)
                P, pk = fm_proj(lambda k: wins[:, k, OFF_KA:OFF_KA + 128], b)
                evac(kT[:, sc * 512:(sc + 1) * 512], P[:], [pk], [("kT", sc)])
                for t in range(4):
                    jt = sc * 4 + t
                    for (c0, n, kind) in ((OFF_VA, 128, "va"), (OFF_IH, 512, "ih"), (OFF_GH, 512, "gh")):
                        P, pk = bank()
                        for k in range(8):
                            S.add("pe", lambda e, P=P, k=k, c0=c0, n=n, t=t, b=b: e.matmul(
                                out=P[:, 0:n], lhsT=xT[b][:, k, t * 128:(t + 1) * 128], rhs=wins[:, k, c0:c0 + n],
                                start=(k == 0), stop=(k == 7)),
                                reads=["wins", ("xT", b, t)], writes=[pk])
                        if kind == "va":
                            evac(vaug[:, jt, :, 0:64], P[:, 0:128].rearrange("p (g d) -> p g d", g=2), [pk], [("vaug", jt)])
                        elif kind == "ih":
                            evac(vtok[b][:, t, :], P[:], [pk], [("vtok", b, t)])
                        else:
                            evac(gtok[t % 2][:], P[:], [pk], [("gtok", t % 2)])
                            rg = t0 + sc * 512 + t * 128
                            dma(lambda e, rg=rg, t=t: e.dma_start(out=scr["G"][rg:rg + 128, :], in_=gtok[t % 2][:]),
                                reads=[("gtok", t % 2)])
                rows = slice(t0 + sc * 512, t0 + (sc + 1) * 512)
                dma(lambda e, rows=rows, b=b: e.dma_start(out=scr["V"][rows, :].rearrange("(t p) c -> p t c", p=128), in_=vtok[b][:]),
                    reads=[("vtok", b, t_) for t_ in range(4)])
                for h in range(4):
                    P, pk = fm_proj(lambda k, h=h: wins[:, k, OFF_QH + h * 128:OFF_QH + (h + 1) * 128], b)
                    evac(qsb[b][:, h, :], P[:], [pk], [("qsb", b, h)])
                for d_ in range(2):
                    foff = OFF_FF if d_ == 0 else OFF_FB
                    for h in range(4):
                        tb = (d_ * 4 + h) % NTMP
                        t1, t2, t3 = tmp[tb]
                        tk = lambda i, tb=tb: ("tmp", tb, i)
                        P, pk = fm_proj(lambda k, h=h, foff=foff: wins[:, k, foff + h * 128:foff + (h + 1) * 128], b, "g")
                        lbc, lnc = lb[:, d_, h:h + 1], lnoml[:, d_, h:h + 1]
                        cpos = 63 if d_ == 0 else 0
                        t1v = t1[:].rearrange("p (c t) -> p c t", c=8)
                        t2v = t2[:].rearrange("p (c t) -> p c t", c=8)
                        t3v = t3[:].rearrange("p (c t) -> p c t", c=8)
                        S.add("act", lambda e, P=P, t1=t1: e.activation(out=t1[:], in_=P[:], func=AF.Exp, scale=-1.0),
                              reads=[pk], writes=[tk(1)])
                        S.add("act", lambda e, t1=t1, t2=t2: e.activation(out=t2[:], in_=t1[:], func=AF.Ln, bias=one[:, 0:1], scale=1.0),
                              reads=[tk(1), "one"], writes=[tk(2)])
                        S.add("act", lambda e, t1=t1, t3=t3, lbc=lbc: e.activation(out=t3[:], in_=t1[:], func=AF.Ln, bias=one[:, 0:1], scale=lbc),
                              reads=[tk(1), "one", "lb"], writes=[tk(3)])
                        S.add("dve", lambda e, t2=t2, t3=t3: e.tensor_tensor(out=t3[:], in0=t3[:], in1=t2[:], op=ALU.subtract),
                              reads=[tk(2)], writes=[tk(3)])
                        S.add("dve", lambda e, P=P, t2=t2: e.tensor_tensor(out=t2[:], in0=P[:], in1=t2[:], op=ALU.add),
                              reads=[pk], writes=[tk(2)])
                        S.add("dve", lambda e, t1=t1, t3=t3: e.tensor_tensor_scan(
                            out=t1[:], data0=mreset[:], data1=t3[:], initial=0.0, op0=ALU.mult, op1=ALU.add),
                            reads=[tk(3), "mreset"], writes=[tk(1)])
                        if d_ == 1:
                            S.add("dve", lambda e, t1v=t1v, tb=tb: e.tensor_copy(out=tot[tb][:], in_=t1v[:, :, 63]),
                                  reads=[tk(1)], writes=[("tot", tb)])
                            S.add("dve", lambda e, t1=t1, t3=t3: e.scalar_tensor_tensor(
                                out=t1[:], in0=t1[:], scalar=-1.0, in1=t3[:], op0=ALU.mult, op1=ALU.add),
                                reads=[tk(3)], writes=[tk(1)])
                            S.add("dve", lambda e, t1v=t1v, tb=tb: e.tensor_tensor(
                                out=t1v, in0=t1v, in1=tot[tb][:, :, None].broadcast_to([128, 8, 64]), op=ALU.add),
                                reads=[("tot", tb)], writes=[tk(1)])
                        S.add("act", lambda e, t1=t1, t3=t3: e.activation(out=t3[:], in_=t1[:], func=AF.Exp),
                              reads=[tk(1)], writes=[tk(3)])
                        S.add("dve", lambda e, d_=d_, h=h, t3=t3, b=b: e.tensor_tensor(out=qa[d_][:, h, :], in0=qsb[b][:, h, :], in1=t3[:],
                                                                                     op=ALU.mult),
                              reads=[("qsb", b, h), tk(3)], writes=[("qa", d_, h)])
                        S.add("dve", lambda e, d_=d_, h=h, t3v=t3v, cpos=cpos: e.tensor_copy(out=c1t[d_][:, h, :], in_=t3v[:, :, cpos]),
                              reads=[tk(3)], writes=[("c1t", d_, h)])
                        S.add("dve", lambda e, t1=t1, t2=t2: e.tensor_tensor(out=t2[:], in0=t2[:], in1=t1[:], op=ALU.add),
                              reads=[tk(1)], writes=[tk(2)])
                        S.add("act", lambda e, d_=d_, h=h, t2=t2, lnc=lnc: e.activation(
                            out=ka[d_][:, h, :], in_=t2[:], func=AF.Exp, bias=lnc, scale=-1.0),
                            reads=[tk(2), "lnoml"], writes=[("ka", d_, h)])
                        S.add("dve", lambda e, t1v=t1v, t2v=t2v, cpos=cpos: e.tensor_tensor(
                            out=t2v, in0=t2v, in1=t1v[:, :, cpos:cpos + 1].broadcast_to([128, 8, 64]), op=ALU.subtract),
                            reads=[tk(1)], writes=[tk(2)])
                        S.add("act", lambda e, d_=d_, h=h, t2=t2, lnc=lnc: e.activation(
                            out=kdT[d_][:, h, :], in_=t2[:], func=AF.Exp, bias=lnc, scale=-1.0),
                            reads=[tk(2), "lnoml"], writes=[("kdT", d_, h)])
                    for t in range(4):
                        P, pk = bank()
                        P16 = P[:].bitcast(BF16)[:, 0:512]
                        Pv = P16.rearrange("p (a b) -> p a b", a=4)
                        for h in range(4):
                            S.add("pe", lambda e, Pv=Pv, h=h, t=t, d_=d_: e.transpose(
                                out=Pv[:, h, :], in_=kdT[d_][:, h, t * 128:(t + 1) * 128], identity=ident16[:]),
                                reads=[("kdT", d_, h), "ident16"], writes=[pk])
                        evac(kdtok[d_][:, t, :], P16, [pk], [("kdtok", d_, t)])
                    dma(lambda e, d_=d_, gsc=gsc: e.dma_start(out=scr["QA"][d_][gsc], in_=qa[d_][:]),
                        reads=[("qa", d_, h_) for h_ in range(4)])
                    dma(lambda e, d_=d_, gsc=gsc: e.dma_start(out=scr["KA"][d_][gsc], in_=ka[d_][:]),
                        reads=[("ka", d_, h_) for h_ in range(4)])
                    dma(lambda e, d_=d_, gsc=gsc: e.dma_start(out=scr["C1"][d_][gsc], in_=c1t[d_][:]),
                        reads=[("c1t", d_, h_) for h_ in range(4)])
                    dma(lambda e, d_=d_, rows=rows: e.dma_start(
                        out=scr["KD"][d_][rows, :].rearrange("(t p) c -> p t c", p=128), in_=kdtok[d_][:]),
                        reads=[("kdtok", d_, t_) for t_ in range(4)])
                jlo = max(sc * 4 - 1, 0)
                jhi = sc * 4 + 3 if sc < nsc - 1 else sc * 4 + 4
                for j in range(jlo, jhi):
                    attention_tile(t0, j, ntiles)
                gsc += 1
            t0 += T
        S.barrier()


def scan_phase(nc, S, cst, scr, seqs, tag="s_", prefetch=None):
    with ExitStack() as st:
        sb = lambda name, shape, dt: st.enter_context(nc.sbuf_tensor(tag + name, shape, dt))
        ps = lambda name, shape, dt: st.enter_context(nc.psum_tensor(tag + name, shape, dt))
        cmask = sb("cmask", [64, 2, 256], F32)
        qa_s = [[sb("qa%d_%d" % (d_, i), [128, 4, 512], BF16) for i in range(2)] for d_ in range(2)]
        ka_s = [[sb("ka%d_%d" % (d_, i), [128, 4, 512], BF16) for i in range(2)] for d_ in range(2)]
        kd_s = [[sb("kd%d_%d" % (d_, i), [64, 8, 512], BF16) for i in range(2)] for d_ in range(2)]
        v_s = [[sb("v%d_%d" % (d_, i), [64, 8, 512], BF16) for i in range(2)] for d_ in range(2)]
        c1_s = [[sb("c1%d_%d" % (d_, i), [128, 4, 8], F32) for i in range(2)] for d_ in range(2)]
        S32 = [sb("S32_%d" % d_, [128, 512], F32) for d_ in range(2)]
        S16 = [sb("S16_%d" % d_, [128, 512], BF16) for d_ in range(2)]
        at16 = [sb("at16_%d" % d_, [64, 256], BF16) for d_ in range(2)]
        osb = [[sb("osb%d_%d" % (d_, i), [64, 512], F32) for i in range(2)] for d_ in range(2)]
        PA = [ps("PA%d" % d_, [128, 512], F32) for d_ in range(2)]
        PO = [ps("PO%d" % d_, [128, 512], F32) for d_ in range(2)]
        PU = [ps("PU%d" % d_, [128, 512], F32) for d_ in range(2)]
        PO2 = [ps("PO2_%d" % d_, [128, 512], F32) for d_ in range(2)]
        dma = lambda fn, reads=(), writes=(): S.add("sp", fn, reads=reads, writes=writes, dma=True)
        dma(lambda e: e.dma_start(out=cmask[:].rearrange("p a b -> p (a b)"), in_=cst["chunkmask"]), writes=["cmask"])
        if prefetch is not None:
            prefetch()
        OUT = (scr["OF"], scr["OB"])
        gsc0 = 0
        t0 = 0
        nld = [0, 0]
        for T in seqs:
            nsc = T // 512
            for d_ in range(2):
                S.add("pool", lambda e, d_=d_: e.memset(S32[d_][:], 0.0), writes=[("S32", d_)])
                S.add("pool", lambda e, d_=d_: e.memset(S16[d_][:], 0.0), writes=[("S16", d_)])

            def load(d_, sc):
                bi = nld[d_] % 2
                nld[d_] += 1
                gsc = gsc0 + sc
                rows = slice(t0 + sc * 512, t0 + (sc + 1) * 512)
                k = ("in", d_, bi)
                dma(lambda e: e.dma_start(out=qa_s[d_][bi][:], in_=scr["QA"][d_][gsc]), writes=[k])
                dma(lambda e: e.dma_start(out=ka_s[d_][bi][:], in_=scr["KA"][d_][gsc]), writes=[k])
                dma(lambda e: e.dma_start(out=c1_s[d_][bi][:], in_=scr["C1"][d_][gsc]), writes=[k])
                dma(lambda e: e.dma_start(out=kd_s[d_][bi][:], in_=scr["KD"][d_][rows, :].rearrange("(c p) f -> p c f", p=64)),
                    writes=[k])
                dma(lambda e: e.dma_start(out=v_s[d_][bi][:], in_=scr["V"][rows, :].rearrange("(c p) f -> p c f", p=64)),
                    writes=[k])
                return bi

            pend = [load(0, 0), load(1, nsc - 1)]
            for i in range(nsc):
                cur = pend
                scs = (i, nsc - 1 - i)
                if i + 1 < nsc:
                    pend = [load(0, i + 1), load(1, nsc - 2 - i)]
                for cc in range(8):
                    for d_ in range(2):
                        bi = cur[d_]
                        c = cc if d_ == 0 else 7 - cc
                        cs = slice(c * 64, (c + 1) * 64)
                        ink = ("in", d_, bi)
                        ob = cc % 2
                        for h in range(4):
                            S.add("pe", lambda e, d_=d_, bi=bi, h=h, cs=cs: e.matmul(
                                out=PA[d_][0:64, h * 64:(h + 1) * 64], lhsT=ka_s[d_][bi][:, h, cs], rhs=qa_s[d_][bi][:, h, cs],
                                start=True, stop=True), reads=[ink], writes=[("PA", d_)])
                        S.add("dve", lambda e, d_=d_: e.tensor_tensor(out=at16[d_][:], in0=PA[d_][0:64, 0:256], in1=cmask[:, d_, :],
                                                                    op=ALU.mult),
                              reads=[("PA", d_), "cmask"], writes=[("at16", d_)])
                        for pair in ((0, 2), (1, 3)):
                            for h in pair:
                                hs = slice(h * 128, (h + 1) * 128)
                                Pb, pkey = (PO[d_], ("PO", d_)) if h < 2 else (PO2[d_], ("PO2", d_))
                                os_ = slice((h % 2) * 128, (h % 2) * 128 + 128)
                                S.add("pe", lambda e, d_=d_, bi=bi, h=h, hs=hs, cs=cs, Pb=Pb, os_=os_: e.matmul(
                                    out=Pb[0:64, os_], lhsT=qa_s[d_][bi][:, h, cs], rhs=S16[d_][:, hs],
                                    start=True, stop=False), reads=[ink, ("S16", d_)], writes=[pkey])
                            for h in pair:
                                hs = slice(h * 128, (h + 1) * 128)
                                Pb, pkey = (PO[d_], ("PO", d_)) if h < 2 else (PO2[d_], ("PO2", d_))
                                os_ = slice((h % 2) * 128, (h % 2) * 128 + 128)
                                S.add("pe", lambda e, d_=d_, bi=bi, h=h, hs=hs, c=c, Pb=Pb, os_=os_: e.matmul(
                                    out=Pb[0:64, os_], lhsT=at16[d_][:, h * 64:(h + 1) * 64], rhs=v_s[d_][bi][:, c, hs],
                                    start=False, stop=True), reads=[ink, ("at16", d_)], writes=[pkey])
                        for h in range(4):
                            hs = slice(h * 128, (h + 1) * 128)
                            S.add("pe", lambda e, d_=d_, bi=bi, hs=hs, c=c: e.matmul(
                                out=PU[d_][:, hs], lhsT=kd_s[d_][bi][:, c, hs], rhs=v_s[d_][bi][:, c, hs],
                                start=True, stop=True), reads=[ink], writes=[("PU", d_)])
                        for (Pb, pkey, half) in ((PO[d_], ("PO", d_), 0), (PO2[d_], ("PO2", d_), 1)):
                            o_ = osb[d_][ob][:, half * 256:(half + 1) * 256]
                            i_ = Pb[0:64, 0:256]
                            S.add(("act", "dve"), {"act": lambda e, o_=o_, i_=i_: e.activation(out=o_, in_=i_, func=AF.Copy),
                                                   "dve": lambda e, o_=o_, i_=i_: e.tensor_copy(out=o_, in_=i_)},
                                  reads=[pkey], writes=[("osb", d_, ob, half)])
                        r0 = t0 + scs[d_] * 512 + c * 64
                        sc_mine = scs[d_]
                        first = (sc_mine < nsc // 2) if d_ == 0 else (sc_mine >= nsc // 2)
                        rk = ("OFrow", r0)
                        if first:
                            dma(lambda e, ob=ob, d_=d_, r0=r0: e.dma_start(out=scr["OF"][r0:r0 + 64, :], in_=osb[d_][ob][:]),
                                reads=[("osb", d_, ob, 0), ("osb", d_, ob, 1)], writes=[rk])
                        else:
                            S.add("pool", lambda e, ob=ob, d_=d_, r0=r0: e.dma_start(
                                out=scr["OF"][r0:r0 + 64, :], in_=osb[d_][ob][:], accum_op=ALU.add),
                                reads=[("osb", d_, ob, 0), ("osb", d_, ob, 1)], writes=[rk], dma=True)
                        for h in range(4):
                            hs = slice(h * 128, (h + 1) * 128)
                            S.add("dve", lambda e, d_=d_, bi=bi, h=h, hs=hs, c=c: e.scalar_tensor_tensor(
                                out=S32[d_][:, hs], in0=S32[d_][:, hs], scalar=c1_s[d_][bi][:, h, c:c + 1], in1=PU[d_][:, hs],
                                op0=ALU.mult, op1=ALU.add), reads=[ink, ("PU", d_)], writes=[("S32", d_)])
                        S.add("act", lambda e, d_=d_: e.activation(out=S16[d_][:], in_=S32[d_][:], func=AF.Copy),
                              reads=[("S32", d_)], writes=[("S16", d_)])
            gsc0 += nsc
            t0 += T
        S.barrier()


def combine_phase(nc, S, X1, X2, w_out, hg_g, lng, lnb, cst, scr, ntok, tag="c_", prefetch=None):
    NB = 3
    with ExitStack() as st:
        sb = lambda name, shape, dt: st.enter_context(nc.sbuf_tensor(tag + name, shape, dt))
        ps = lambda name, shape, dt: st.enter_context(nc.psum_tensor(tag + name, shape, dt))
        wouts = sb("wouts", [128, 8, D], BF16)
        ident = sb("ident", [128, 128], F32)
        ident16 = sb("ident16", [128, 128], BF16)
        gt = sb("gt", [128, D], F32)
        bt = sb("bt", [128, D], F32)
        hgT = sb("hgT", [128, 4], F32)
        xr = [sb("xr%d" % i, [128, D], F32) for i in range(NB)]
        of = [sb("of%d" % i, [128, 512], F32) for i in range(NB)]
        gg = [sb("gg%d" % i, [128, 512], F32) for i in range(NB)]
        at = [sb("at%d" % i, [128, 512], BF16) for i in range(NB)]
        mixh = [sb("mixh%d" % i, [128, 512], BF16) for i in range(2)]
        mixT = [sb("mixT%d" % i, [128, 8, 128], BF16) for i in range(2)]
        junk = sb("junk", [128, 128], BF16)
        mhalf = sb("mhalf", [128, 8], F32)
        ssq = [sb("ssq%d" % i, [128, 4], F32) for i in range(NB)]
        stats = [sb("st%d" % i, [128, 2, 6], F32) for i in range(NB)]
        mv = [sb("mv%d" % i, [128, 2], F32) for i in range(NB)]
        rstd = [sb("rs%d" % i, [128, 1], F32) for i in range(NB)]
        nmr = [sb("nm%d" % i, [128, 1], F32) for i in range(NB)]
        PT = [ps("PT%d" % i, [128, 8, 128], BF16) for i in range(2)]
        PY = [ps("PY%d" % i, [128, D], F32) for i in range(NB)]
        dma = lambda fn, reads=(), writes=(): S.add("sp", fn, reads=reads, writes=writes, dma=True)
        dma(lambda e: e.dma_start(out=ident[:], in_=cst["ident"]), writes=["ident"])
        dma(lambda e: e.dma_start(out=hgT[:], in_=hg_g.rearrange("o (k p) -> p (o k)", p=128),
                                  allow_slow_non_contiguous=True), writes=["hgT"])
        S.add("pool", lambda e: e.memset(mhalf[:], -0.5), writes=["mhalf"])
        load_w_bf16(S, wouts, w_out, 8, "wouts")
        if prefetch is not None:
            prefetch()
        S.add("act", lambda e: e.activation(out=ident16[:], in_=ident[:], func=AF.Copy), reads=["ident"], writes=["ident16"])
        for k in range(4, 8):
            S.add("act", lambda e, k=k: e.activation(out=wouts[:, k, :], in_=wouts[:, k, :], func=AF.Copy, scale=hgT[:, k - 4:k - 3]),
                  reads=["hgT"] + [("wouts", k_) for k_ in range(8)], writes=["wouts"])
        for it in range(ntok // 128):
            b = it % NB
            b2 = it % 2
            rows = slice(it * 128, (it + 1) * 128)
            dma(lambda e, b=b, rows=rows: e.dma_start(out=of[b][:], in_=scr["OF"][rows, :]), writes=[("of", b)])
            dma(lambda e, b=b, rows=rows: e.dma_start(out=gg[b][:], in_=scr["G"][rows, :]), writes=[("gg", b)])
            dma(lambda e, b=b, rows=rows: e.dma_start(out=at[b][:], in_=scr["AT"][rows, :]), writes=[("at", b)])
            dma(lambda e, b=b, rows=rows: e.dma_start(out=xr[b][:], in_=X1[rows, :]), writes=[("xr", b)])
            for h in range(4):
                hs = slice(h * 128, (h + 1) * 128)
                S.add("act", lambda e, b=b, h=h, hs=hs: e.activation(out=junk[:], in_=of[b][:, hs], func=AF.Square,
                                                                    accum_out=ssq[b][:, h:h + 1]),
                      reads=[("of", b)], writes=[("ssq", b, h), "junk"])
            S.add("dve", lambda e, b=b: e.tensor_scalar(out=ssq[b][:], in0=ssq[b][:], scalar1=1.0 / 128, scalar2=EPS,
                                                        op0=ALU.mult, op1=ALU.add),
                  reads=[("ssq", b, h) for h in range(4)], writes=[("ssq", b)])
            S.add("pool", lambda e, b=b: e.tensor_tensor(out=ssq[b][:], in0=ssq[b][:], in1=mhalf[:, 0:4], op=ALU.pow),
                  reads=["mhalf"], writes=[("ssq", b)])
            S.add("act", lambda e, b=b: e.activation(out=gg[b][:], in_=gg[b][:], func=AF.Silu), writes=[("gg", b)])
            for h in range(4):
                hs = slice(h * 128, (h + 1) * 128)
                S.add("dve", lambda e, b=b, b2=b2, h=h, hs=hs: e.scalar_tensor_tensor(
                    out=mixh[b2][:, hs], in0=of[b][:, hs], scalar=ssq[b][:, h:h + 1], in1=gg[b][:, hs], op0=ALU.mult, op1=ALU.mult),
                    reads=[("ssq", b), ("of", b), ("gg", b)], writes=[("mixh", b2)])
            for k in range(8):
                if k < 4:
                    S.add("pe", lambda e, k=k, b=b, b2=b2: e.transpose(
                        out=PT[b2][:, k, :], in_=at[b][:, k * 128:(k + 1) * 128], identity=ident16[:]),
                        reads=[("at", b), "ident16"], writes=[("PT", b2)])
                else:
                    S.add("pe", lambda e, k=k, b2=b2: e.transpose(
                        out=PT[b2][:, k, :], in_=mixh[b2][:, (k - 4) * 128:(k - 3) * 128], identity=ident16[:]),
                        reads=[("mixh", b2), "ident16"], writes=[("PT", b2)])
            S.add("act", lambda e, b2=b2: e.activation(out=mixT[b2][:], in_=PT[b2][:], func=AF.Copy),
                  reads=[("PT", b2)], writes=[("mixT", b2)])
            for h in range(2):
                for k in range(8):
                    S.add("pe", lambda e, h=h, k=k, b=b, b2=b2: e.matmul(
                        out=PY[b][:, h * 512:(h + 1) * 512], lhsT=mixT[b2][:, k, :], rhs=wouts[:, k, h * 512:(h + 1) * 512],
                        start=(k == 0), stop=(k == 7)),
                        reads=["wouts", ("mixT", b2)], writes=[("PY", b, h)])
            layer_norm_tile(S, xr[b], ("xr", b), [PY[b][:, 0:512], PY[b][:, 512:1024]], [("PY", b, 0), ("PY", b, 1)],
                            ALPHA, EPS, gt, bt, stats[b], mv[b], rstd[b], nmr[b], ("lnst", b), mhalf, affine=False)
            dma(lambda e, b=b, rows=rows: e.dma_start(out=X2[rows, :], in_=xr[b][:]), reads=[("xr", b)])
        S.barrier()


def build(seqs, debug=False, upto=5):
    ntok = sum(seqs)
    nsct = ntok // 512
    nc = bass.Bass("TRN2", target_bir_lowering=False)
    dr = lambda name, shape, dt, kind="ExternalInput": nc.dram_tensor(name, shape, dt, kind=kind).ap()
    x = dr("x", [ntok, D], F32)
    ln_g = dr("ln_g", [3, D], F32)
    ln_b = dr("ln_b", [3, D], F32)
    w13 = dr("ffn_w13", [2, D, 2 * DFF], F32)
    w2 = dr("ffn_w2", [2, DFF, D], F32)
    w_in = dr("w_in", [D, INC], F32)
    w_out = dr("w_out", [D, D], F32)
    sink = dr("attn_sink", [1, 8], F32)
    attn_g = dr("attn_norm_g", [1, 512], F32)
    lbf = dr("hg_lb_fwd", [2, 512], F32)
    lbb = dr("hg_lb_bwd", [2, 512], F32)
    hg_g = dr("hg_norm_g", [1, 512], F32)
    cst = {"ident": dr("ident", [128, 128], F32), "maskreset": dr("maskreset", [128, 512], F32),
           "attbias": dr("attbias", [128, 3072], F32), "chunkmask": dr("chunkmask", [64, 512], F32)}
    y = dr("y", [ntok, D], F32, "ExternalOutput")
    sk = "ExternalOutput" if debug else "Internal"
    scr = {
        "QA": [dr("QA%d" % d_, [nsct, 128, 4, 512], BF16, sk) for d_ in range(2)],
        "KA": [dr("KA%d" % d_, [nsct, 128, 4, 512], BF16, sk) for d_ in range(2)],
        "C1": [dr("C1%d" % d_, [nsct, 128, 4, 8], F32, sk) for d_ in range(2)],
        "KD": [dr("KD%d" % d_, [ntok, 512], BF16, sk) for d_ in range(2)],
        "V": dr("Vs", [ntok, 512], BF16, sk), "G": dr("Gs", [ntok, 512], F32, sk),
        "AT": dr("ATs", [ntok, 512], BF16, sk), "OF": dr("OFs", [ntok, 512], F32, sk), "OB": dr("OBs", [ntok, 512], F32, sk),
    }
    X1 = dr("X1s", [ntok, D], F32, sk)
    X2 = dr("X2s", [ntok, D], F32, sk)
    S = Sched()
    with ExitStack() as st:
        sems = {}
        for c in list(ENGS) + ["dma%d" % i for i in range(S.n_dma_sems)] + ["wdma%d" % i for i in range(S.N_WSEMS)] + ["sdma%d" % i for i in range(S.N_SSEMS)]:
            sems[c] = st.enter_context(nc.semaphore("s_" + c))
        if upto >= 1:
            ffn_phase(nc, S, x, X1, w13[0], w2[0], ln_g[0:1, :], ln_b[0:1, :], cst["ident"], ntok, "a_")
        if upto >= 2:
            proj_phase(nc, S, X1, w_in, sink, attn_g, lbf, lbb, cst, scr, seqs)
        if upto >= 3:
            wb = [st.enter_context(nc.sbuf_tensor("b_w13s", [128, 8, 2 * DFF], BF16)), None]
            scan_phase(nc, S, cst, scr, seqs,
                       prefetch=lambda: ffn_load_weights(S, wb[0], wb[1], w13[1], w2[1], parts=(0,)))
        if upto >= 4:
            wb[1] = st.enter_context(nc.sbuf_tensor("b_w2s", [128, NF, D], BF16))
            combine_phase(nc, S, X1, X2, w_out, hg_g, ln_g[1:2, :], ln_b[1:2, :], cst, scr, ntok,
                          prefetch=lambda: ffn_load_weights(S, wb[0], wb[1], w13[1], w2[1], parts=(1,)))
        if upto >= 5:
            ffn_phase(nc, S, X2, y, w13[1], w2[1], ln_g[2:3, :], ln_b[2:3, :], cst["ident"], ntok, "b_", wbuf=wb,
                      pre=(ln_g[1:2, :], ln_b[1:2, :]))
        S.barrier()
        with nc.Block() as block:
            S.emit(block, sems)
    nc._sched_phase_times = S.phase_times
    return nc


def core_inputs(inputs, xc):
    m = {"x": np.ascontiguousarray(xc, dtype=np.float32)}
    m["ln_g"] = np.ascontiguousarray(inputs["ln_g"][0])
    m["ln_b"] = np.ascontiguousarray(inputs["ln_b"][0])
    m["ffn_w13"] = np.ascontiguousarray(inputs["ffn_w13"][0])
    m["ffn_w2"] = np.ascontiguousarray(inputs["ffn_w2"][0])
    m["w_in"] = np.ascontiguousarray(inputs["w_in"][0])
    m["w_out"] = np.ascontiguousarray(inputs["w_out"][0])
    m["attn_sink"] = np.ascontiguousarray(inputs["attn_sink"]).reshape(1, 8)
    m["attn_norm_g"] = np.ascontiguousarray(inputs["attn_norm_g"]).reshape(1, 512)
    m["hg_lb_fwd"] = np.ascontiguousarray(inputs["hg_lb_fwd"])
    m["hg_lb_bwd"] = np.ascontiguousarray(inputs["hg_lb_bwd"])
    m["hg_norm_g"] = np.ascontiguousarray(inputs["hg_norm_g"]).reshape(1, 512)
    m.update(host_consts())
    return m


def kernel(**inputs):
    inputs = {k: np.asarray(v) for k, v in inputs.items()}
    xp, xsm = inputs["x_prompt"], inputs["x_sample"]
    n = 8
    B, T, _ = xp.shape
    Bs, Ts, _ = xsm.shape
    per_p, per_s = B // n, Bs // n
    seqs = [T] * per_p + [Ts] * per_s
    nc = build(seqs)
    in_maps = []
    for c in range(n):
        parts = [xp[c * per_p + i] for i in range(per_p)] + [xsm[c * per_s + i] for i in range(per_s)]
        in_maps.append(core_inputs(inputs, np.concatenate(parts, axis=0)))
    res = run_bass_kernel_spmd(nc, in_maps, core_ids=list(range(n)))
    yp = np.empty_like(xp, dtype=np.float32)
    ys = np.empty_like(xsm, dtype=np.float32)
    for c in range(n):
        yc = res.results[c]["y"]
        o = 0
        for i in range(per_p):
            yp[c * per_p + i] = yc[o:o + T]
            o += T
        for i in range(per_s):
            ys[c * per_s + i] = yc[o:o + Ts]
            o += Ts
    return (yp, ys)
```
